# Optimizing a Trainium2 kernel written in Bass

```python
import math
import jax, jax.numpy as jnp
from jax import lax
import numpy as np

D_MODEL = 1024
BATCH = 4
SEQ = 8192
DEPTH = 2

CHUNK = 64
HEAD_DIM = 64
N_HEADS_A = 8
N_HEADS_B = 8
WIDTH_A = N_HEADS_A * HEAD_DIM
WIDTH_B = N_HEADS_B * HEAD_DIM
LEFT_CHUNKS = 8
BAND = (LEFT_CHUNKS + 1) * CHUNK
REL_CLIP = 256
N_REL = 2 * REL_CLIP + 1
SB_BLOCK = 128
D_FF = -(-8 * D_MODEL // (3 * 256)) * 256
IN_COLS = 3 * WIDTH_A + 3 * WIDTH_B + 2 * D_MODEL
DEEPNORM_ALPHA = (2 * DEPTH) ** 0.25
DEEPNORM_BETA = (8 * DEPTH) ** -0.25
LN_EPS = 1e-5

kernel_name = "hybrid_chunk_relbias_stickbreaking_deepnorm"


def layer_norm(x, g, b):
    xf = x.astype(jnp.float32)
    mu = jnp.mean(xf, axis=-1, keepdims=True)
    var = jnp.mean(jnp.square(xf - mu), axis=-1, keepdims=True)
    y = (xf - mu) * lax.rsqrt(var + LN_EPS) * g.astype(jnp.float32) + b.astype(jnp.float32)
    return y.astype(x.dtype)


def chunk_band_attention(q, k, v, rel_bias):
    B, S, H, dh = q.shape
    nc = S // CHUNK
    qc = q.reshape(B, nc, CHUNK, H, dh)
    pad = ((0, 0), (LEFT_CHUNKS * CHUNK, 0), (0, 0), (0, 0))
    kp = jnp.pad(k, pad).reshape(B, nc + LEFT_CHUNKS, CHUNK, H, dh)
    vp = jnp.pad(v, pad).reshape(B, nc + LEFT_CHUNKS, CHUNK, H, dh)
    kb = jnp.concatenate([kp[:, j:j + nc] for j in range(LEFT_CHUNKS + 1)], axis=2)
    vb = jnp.concatenate([vp[:, j:j + nc] for j in range(LEFT_CHUNKS + 1)], axis=2)
    scores = jnp.einsum('bcqhd,bckhd->bhcqk', qc, kb).astype(jnp.float32) / math.sqrt(dh)
    i = jnp.arange(CHUNK)[:, None]
    p = jnp.arange(BAND)[None, :]
    dist = LEFT_CHUNKS * CHUNK + i - p
    idx = jnp.clip(dist, -REL_CLIP, REL_CLIP) + REL_CLIP
    bias = rel_bias.astype(jnp.float32)[:, idx]
    valid = (jnp.arange(nc)[:, None] + jnp.arange(BAND)[None, :] // CHUNK - LEFT_CHUNKS) >= 0
    scores = scores + bias[None, :, None, :, :]
    scores = jnp.where(valid[None, None, :, None, :], scores, -jnp.inf)
    probs = jax.nn.softmax(scores, axis=-1).astype(v.dtype)
    out = jnp.einsum('bhcqk,bckhd->bcqhd', probs, vb)
    return out.reshape(B, S, H * dh)


def stick_breaking_attention(q, k, v):
    B, S, H, dh = q.shape
    nb = S // SB_BLOCK
    scale = 1.0 / math.sqrt(dh)
    qb = q.reshape(B, nb, SB_BLOCK, H, dh).transpose(1, 0, 2, 3, 4)
    key_pos = jnp.arange(S)

    def one_block(args):
        q_blk, blk = args
        z = jnp.einsum('bqhd,bshd->bhqs', q_blk, k).astype(jnp.float32) * scale
        t = blk * SB_BLOCK + jnp.arange(SB_BLOCK)
        causal = (key_pos[None, :] < t[:, None])[None, None]
        log_keep = jnp.where(causal, jax.nn.log_sigmoid(-z), 0.0)
        suffix = lax.cumsum(log_keep, axis=3, reverse=True) - log_keep
        log_w = jnp.where(causal, jax.nn.log_sigmoid(z) + suffix, -jnp.inf)
        w = jnp.exp(log_w).astype(v.dtype)
        return jnp.einsum('bhqs,bshd->bqhd', w, v)

    out = lax.map(one_block, (qb, jnp.arange(nb)))
    return out.transpose(1, 0, 2, 3, 4).reshape(B, S, H * dh)


def setup_inputs(seed: int = 0) -> dict:
    key = jax.random.key(seed)
    ks = jax.random.split(key, 16)
    D = D_MODEL
    x = jax.random.normal(ks[0], (BATCH, SEQ, D), jnp.float32)
    w_in = jax.random.normal(ks[1], (DEPTH, D, IN_COLS), jnp.float32) * D ** -0.5
    col_scale = jnp.concatenate([
        jnp.ones((2 * WIDTH_A,)), jnp.full((WIDTH_A,), DEEPNORM_BETA),
        jnp.ones((2 * WIDTH_B,)), jnp.full((WIDTH_B,), DEEPNORM_BETA),
        jnp.ones((2 * D,))]).astype(jnp.float32)
    w_in = w_in * col_scale
    b_gate = 0.01 * jax.random.normal(ks[2], (DEPTH, 2 * D), jnp.float32)
    rel_bias = 0.1 * jax.random.normal(ks[3], (DEPTH, N_HEADS_A, N_REL), jnp.float32)
    w_proj_a = jax.random.normal(ks[4], (DEPTH, WIDTH_A, D), jnp.float32) * WIDTH_A ** -0.5
    w_proj_b = jax.random.normal(ks[5], (DEPTH, WIDTH_B, D), jnp.float32) * WIDTH_B ** -0.5
    w_out = jax.random.normal(ks[6], (DEPTH, D, D), jnp.float32) * (D ** -0.5 * DEEPNORM_BETA)
    ln1_g = 1.0 + 0.02 * jax.random.normal(ks[7], (DEPTH, D), jnp.float32)
    ln1_b = 0.02 * jax.random.normal(ks[8], (DEPTH, D), jnp.float32)
    w_ffn_in = jax.random.normal(ks[9], (DEPTH, D, 2 * D_FF), jnp.float32) * D ** -0.5
    w_ffn_out = jax.random.normal(ks[10], (DEPTH, D_FF, D), jnp.float32) * (D_FF ** -0.5 * DEEPNORM_BETA)
    ln2_g = 1.0 + 0.02 * jax.random.normal(ks[11], (DEPTH, D), jnp.float32)
    ln2_b = 0.02 * jax.random.normal(ks[12], (DEPTH, D), jnp.float32)
    return {"x": x, "w_in": w_in, "b_gate": b_gate, "rel_bias": rel_bias,
            "w_proj_a": w_proj_a, "w_proj_b": w_proj_b, "w_out": w_out,
            "ln1_g": ln1_g, "ln1_b": ln1_b, "w_ffn_in": w_ffn_in,
            "w_ffn_out": w_ffn_out, "ln2_g": ln2_g, "ln2_b": ln2_b}


def reference(x, w_in, b_gate, rel_bias, w_proj_a, w_proj_b, w_out,
              ln1_g, ln1_b, w_ffn_in, w_ffn_out, ln2_g, ln2_b):
    B, S, D = x.shape
    split_pts = np.cumsum([WIDTH_A, WIDTH_A, WIDTH_A, WIDTH_B, WIDTH_B, WIDTH_B, D]).tolist()
    for l in range(DEPTH):
        h = x @ w_in[l]
        qa, ka, va, qb, kb, vb, ga, gb = jnp.split(h, split_pts, axis=-1)
        heads_a = lambda t: t.reshape(B, S, N_HEADS_A, HEAD_DIM)
        heads_b = lambda t: t.reshape(B, S, N_HEADS_B, HEAD_DIM)
        y_a = chunk_band_attention(heads_a(qa), heads_a(ka), heads_a(va), rel_bias[l]) @ w_proj_a[l]
        y_b = stick_breaking_attention(heads_b(qb), heads_b(kb), heads_b(vb)) @ w_proj_b[l]
        gate_a = jax.nn.sigmoid(ga + b_gate[l, :D])
        gate_b = jax.nn.sigmoid(gb + b_gate[l, D:])
        mix = (gate_a * y_a + gate_b * y_b) @ w_out[l]
        x = layer_norm(DEEPNORM_ALPHA * x + mix, ln1_g[l], ln1_b[l])
        gu = x @ w_ffn_in[l]
        g, u = jnp.split(gu, 2, axis=-1)
        ffn = (jax.nn.silu(g) * u) @ w_ffn_out[l]
        x = layer_norm(DEEPNORM_ALPHA * x + ffn, ln2_g[l], ln2_b[l])
    return x
```

```python
import contextlib
import numpy as np
import ml_dtypes
import concourse.bass as bass
import concourse.mybir as mybir
from concourse.bass_utils import run_bass_kernel_spmd

F32 = mybir.dt.float32
BF16 = mybir.dt.bfloat16
AF = mybir.ActivationFunctionType
ALU = mybir.AluOpType
AX = mybir.AxisListType

D = 1024
S = 8192
NB = 4
DEPTH = 2
DFF = 2816
INC = 5120
T = 4096
NT = 8
ALPHA = float((2 * DEPTH) ** 0.25)
EPS = 1e-5
NEG = -30000.0

COMPUTE = ("pe", "act", "dve", "pool")
QUEUES = ("sp", "pe", "act", "dve", "pool")
NDMASEM = 16


class Op:
    __slots__ = ("eng", "fn", "deps", "idx", "signal", "semid", "count", "ndma", "barrier", "cc")

    def __init__(self, eng, fn):
        self.eng = eng
        self.fn = fn
        self.deps = set()
        self.signal = False
        self.semid = None
        self.count = 0
        self.ndma = 0
        self.barrier = False
        self.cc = False


class Prog:
    def __init__(self):
        self.ops = []
        self.lastw = {}
        self.readers = {}

    def op(self, eng, fn, reads=(), writes=(), ndma=0, cc=False):
        o = Op(eng, fn)
        o.idx = len(self.ops)
        o.ndma = ndma
        o.cc = cc
        for k in reads:
            w = self.lastw.get(k)
            if w is not None:
                o.deps.add(w)
        for k in writes:
            w = self.lastw.get(k)
            if w is not None:
                o.deps.add(w)
            for r in self.readers.get(k, ()):
                o.deps.add(r)
        for k in reads:
            lst = self.readers.setdefault(k, [])
            if not ndma and not cc:
                lst[:] = [r for r in lst if not (self.ops[r].eng == eng and not self.ops[r].ndma and not self.ops[r].cc)]
            lst.append(o.idx)
        for k in writes:
            self.lastw[k] = o.idx
            self.readers[k] = []
        o.deps.discard(o.idx)
        self.ops.append(o)
        return o

    def barrier(self):
        o = Op("barrier", None)
        o.idx = len(self.ops)
        o.barrier = True
        self.ops.append(o)
        self.lastw = {}
        self.readers = {}

    def dma(self, q, out, in_, reads=(), writes=()):
        return self.op(q, lambda e: [e.dma_start(out=out, in_=in_)], reads, writes, ndma=1)

    def wait_all(self, q, ops_):
        o = Op(q, None)
        o.idx = len(self.ops)
        o.deps = {x.idx for x in ops_}
        self.ops.append(o)
        return o

    def emit(self, nc):
        ops = self.ops
        slot_last = [None] * NDMASEM
        rr = 0
        for o in ops:
            if o.barrier or not o.ndma:
                continue
            o.semid = rr
            if slot_last[rr] is not None:
                o.deps.add(slot_last[rr])
            slot_last[rr] = o.idx
            rr = (rr + 1) % NDMASEM
        last_c = {}
        last_d = [None] * NDMASEM
        last_cc = []
        pending = {q: set() for q in QUEUES}
        for o in ops:
            if o.barrier:
                snap = set(last_c.values()) | {x for x in last_d if x is not None} | set(last_cc)
                for q in QUEUES:
                    pending[q] |= snap
                continue
            if pending[o.eng]:
                o.deps |= pending[o.eng]
                pending[o.eng] = set()
            if o.ndma:
                last_d[o.semid] = o.idx
            elif o.cc:
                last_cc.append(o.idx)
            elif o.fn is not None:
                last_c[o.eng] = o.idx
        for o in ops:
            if o.barrier:
                continue
            if o.eng == "pe" and not o.ndma:
                o.deps = {d for d in o.deps if not (ops[d].eng == "pe" and not ops[d].ndma)}
        for o in ops:
            if o.barrier:
                continue
            for d in o.deps:
                ops[d].signal = True
        cnt = {e: 0 for e in COMPUTE}
        dcnt = [0] * NDMASEM
        ncc = 0
        for o in ops:
            if o.barrier or o.fn is None:
                continue
            if o.ndma:
                dcnt[o.semid] += 16 * o.ndma
                o.count = dcnt[o.semid]
            elif o.cc:
                o.semid = ncc
                ncc += 1
                o.count = 1
            elif o.signal:
                cnt[o.eng] += 1
                o.count = cnt[o.eng]
        self.final_counts = (dict(cnt), list(dcnt))
        nwaits = [0]
        with contextlib.ExitStack() as st:
            sems = {e: st.enter_context(nc.semaphore("s_" + e)) for e in COMPUTE}
            dsems = [st.enter_context(nc.semaphore("d_%d" % i)) for i in range(NDMASEM)]
            csems = [st.enter_context(nc.semaphore("c_%d" % i)) for i in range(ncc)]
            block = st.enter_context(nc.Block())

            def run_engine(engname, e):
                waited = {}
                for o in ops:
                    if o.barrier or o.eng != engname:
                        continue
                    need = {}
                    for d in o.deps:
                        p = ops[d]
                        if p.ndma:
                            key = ("d", p.semid)
                        elif p.cc:
                            key = ("cc", p.semid)
                        else:
                            key = ("c", p.eng)
                        assert p.count > 0, (p.eng, p.idx)
                        if p.count > need.get(key, 0):
                            need[key] = p.count
                    for key, v in need.items():
                        if waited.get(key, 0) >= v:
                            continue
                        waited[key] = v
                        if key[0] == "d":
                            sem = dsems[key[1]]
                        elif key[0] == "cc":
                            sem = csems[key[1]]
                        else:
                            sem = sems[key[1]]
                        e.wait_ge(sem, v)
                        nwaits[0] += 1
                    if o.fn is None:
                        continue
                    res = o.fn(e)
                    if o.ndma:
                        assert len(res) == o.ndma, (len(res), o.ndma)
                        for ins in res:
                            ins.then_inc(dsems[o.semid], 16)
                    elif o.cc:
                        res.then_inc(csems[o.semid], 1)
                    elif o.signal:
                        res.then_inc(sems[o.eng], 1)

            @block.sync
            def _(e):
                run_engine("sp", e)

            @block.tensor
            def _(e):
                run_engine("pe", e)

            @block.scalar
            def _(e):
                run_engine("act", e)

            @block.vector
            def _(e):
                run_engine("dve", e)

            @block.gpsimd
            def _(e):
                run_engine("pool", e)
        self.nwaits = nwaits[0]


class SbAlloc:
    def __init__(self, nc, base=16640, limit=229000):
        self.nc = nc
        self.base = base
        self.off = base
        self.limit = limit
        self.n = 0

    def reset(self, to=None):
        self.off = self.base if to is None else to

    def alloc(self, shape, dtype, name=None):
        esz = 4 if dtype == F32 else 2
        nbytes = esz * int(np.prod(shape[1:]))
        nbytes = (nbytes + 63) // 64 * 64
        assert self.off + nbytes <= self.limit, ("SBUF overflow", name, self.off, nbytes)
        self.n += 1
        t = self.nc.alloc_sbuf_tensor_at("%s_%d" % (name or "t", self.n), list(shape), dtype, offset=self.off)
        self.off += nbytes
        return t


class Ctx:
    pass


def load_cast_weight(K, src2d, dst, nk, ncols, stg, keyname, split):
    P = K.P
    for k in range(nk):
        P.dma("pool", dst[:, k, :], src2d[k * 128:(k + 1) * 128, :], writes=[(keyname, k)])


def phase_proj(K, l):
    import os
    nc, P, sb, ps = K.nc, K.P, K.sb, K.ps
    sb.reset(K.sb_phase_base)
    wbf = sb.alloc([128, 8, INC], BF16, "wbf")
    bg = sb.alloc([128, 16], F32, "bg")
    xf = [sb.alloc([128, 8, 512], F32, "xf") for _ in range(2)]
    xb = [sb.alloc([128, 8, 512], BF16, "xb") for _ in range(NT)]
    ofm = [sb.alloc([128, 4, 512], BF16, "ofm") for _ in range(3)]
    otm = [sb.alloc([128, 4, 512], BF16, "otm") for _ in range(2)]
    xsrc = K.xT if l == 0 else K.x2T
    P.dma("sp", bg[:], K.bgate[l], writes=["bg"])

    def load_x(i):
        P.dma("sp", xf[i % 2][:], xsrc[:, i * 512:(i + 1) * 512].rearrange("(k p) t -> p k t", p=128), writes=[("xf", i % 2)])
        P.op("dve", lambda e, i=i: e.tensor_copy(out=xb[i][:], in_=xf[i % 2][:]),
             reads=[("xf", i % 2)], writes=[("xb", i)])

    load_x(0)
    load_cast_weight(K, K.w_in[l], wbf, 8, INC, None, "wbf", 2)
    st = dict(nev=0, bank=0, og=0)

    def fm_group(i, col0, kind):
        tcols = slice(i * 512, (i + 1) * 512)
        ob = st["og"] % 3
        st["og"] += 1
        for c4 in range(4):
            bank = st["bank"] % 4
            st["bank"] += 1
            pst = ps[:, bank * 512:(bank + 1) * 512]
            c0 = col0 + c4 * 128
            for k in range(8):
                P.op("pe", lambda e, pst=pst, k=k, c0=c0, i=i: e.matmul(
                    pst, wbf[:, k, c0:c0 + 128], xb[i][:, k, :], start=(k == 0), stop=(k == 7)),
                    reads=[("wbf", k), ("xb", i)], writes=[("ps", bank)])
            dst = ofm[ob][:, c4, :]
            if kind in ("qa", "qb", "ka", "kb"):
                sc = 0.125 if kind[0] == "q" else 1.0
                if st["nev"] % 2 == 0:
                    P.op("act", lambda e, dst=dst, pst=pst, sc=sc: e.activation(out=dst, in_=pst, func=AF.Identity, scale=sc),
                         reads=[("ps", bank)], writes=[("ofm", ob, c4)])
                else:
                    P.op("dve", lambda e, dst=dst, pst=pst, sc=sc: e.tensor_scalar(out=dst, in0=pst, scalar1=sc, scalar2=None, op0=ALU.mult),
                         reads=[("ps", bank)], writes=[("ofm", ob, c4)])
                st["nev"] += 1
            else:
                bidx = (0 if kind[1] == "a" else 8) + int(kind[2]) * 4 + c4
                P.op("act", lambda e, dst=dst, pst=pst, bidx=bidx: e.activation(out=dst, in_=pst, func=AF.Sigmoid, bias=bg[:, bidx:bidx + 1], scale=1.0),
                     reads=[("ps", bank), "bg"], writes=[("ofm", ob, c4)])
        if kind in ("ka", "kb"):
            cl = K.c_kaT if kind == "ka" else K.c_kbT
            for jj in range(2):
                P.dma("sp", cl[jj][:, tcols].rearrange("(c p) t -> p c t", p=128), ofm[ob][:, 2 * jj:2 * jj + 2, :],
                      reads=[("ofm", ob, c4) for c4 in range(4)], writes=[("dram", kind, i, jj)])
            return
        if kind == "qa":
            dd = K.qaT[:, tcols]
        elif kind == "qb":
            dd = K.qbT[:, tcols]
        elif kind[0:2] == "ga":
            r0 = int(kind[2]) * 512
            dd = K.gaT[r0:r0 + 512, tcols]
        else:
            r0 = int(kind[2]) * 512
            dd = K.gbT[r0:r0 + 512, tcols]
        P.dma("sp", dd.rearrange("(c p) t -> p c t", p=128), ofm[ob][:],
              reads=[("ofm", ob, c4) for c4 in range(4)], writes=[("dram", kind, i)])

    def tm_v(i):
        for wi, (col0, kind) in enumerate([(1024, "va"), (2560, "vb")]):
            for sub in range(4):
                bank = 4 + (st["bank"] % 4)
                st["bank"] += 1
                pst = ps[:, bank * 512:(bank + 1) * 512]
                for k in range(8):
                    P.op("pe", lambda e, pst=pst, k=k, col0=col0, i=i, sub=sub: e.matmul(
                        pst, xb[i][:, k, sub * 128:(sub + 1) * 128], wbf[:, k, col0:col0 + 512], start=(k == 0), stop=(k == 7)),
                        reads=[("wbf", k), ("xb", i)], writes=[("ps", bank)])
                dst = otm[wi][:, sub, :]
                if st["nev"] % 2 == 0:
                    P.op("act", lambda e, dst=dst, pst=pst: e.activation(out=dst, in_=pst, func=AF.Identity, scale=1.0),
                         reads=[("ps", bank)], writes=[("otm", wi, sub)])
                else:
                    P.op("dve", lambda e, dst=dst, pst=pst: e.tensor_copy(out=dst, in_=pst),
                         reads=[("ps", bank)], writes=[("otm", wi, sub)])
                st["nev"] += 1
            cv = (K.c_va if kind == "va" else K.c_vb)[i // 4]
            P.dma("sp", cv[(i % 4) * 512:(i % 4 + 1) * 512, :].rearrange("(s p) f -> p s f", p=128), otm[wi][:],
                  reads=[("otm", wi, sub) for sub in range(4)], writes=[("dram", kind, i)])

    for i in range(NT):
        if i + 1 < NT:
            load_x(i + 1)
        for col0, kind in [(0, "qa"), (512, "ka"), (1536, "qb"), (2048, "kb")]:
            fm_group(i, col0, kind)
        tm_v(i)
    if not os.environ.get("KSKIPCC"):
        for (cten, gten, keys) in K.cc_list:
            P.op("pool", lambda e, cten=cten, gten=gten: e.collective_compute(
                "AllGather", ALU.bypass, replica_groups=[[0, 1], [2, 3], [4, 5], [6, 7]],
                ins=[cten.ap().opt()], outs=[gten.ap().opt()]),
                reads=keys, writes=["gath"], cc=True)
    for i in range(NT):
        for col0, kind in [(3072, "ga0"), (3584, "ga1"), (4096, "gb0"), (4608, "gb1")]:
            fm_group(i, col0, kind)


class GV:
    def __init__(self, K):
        self.K = K

    def va_rows(self, which, r, t0, n):
        g = (self.K.g_va if which == "a" else self.K.g_vb)[t0 // 2048]
        o = r * 2048 + t0 % 2048
        return g[o:o + n, :]

    def kT(self, which, r, j):
        g = (self.K.g_kaT if which == "a" else self.K.g_kbT)[j]
        return g[r * 2048:(r + 1) * 2048, :].rearrange("(f a) b -> f (a b)", f=256)


def phase_attn_a(K, l):
    nc, P, sb, ps = K.nc, K.P, K.sb, K.ps
    sb.reset(K.sb_phase_base)
    TT32 = sb.alloc([128, 8, 640], F32, "TT32")
    MB = sb.alloc([128, 640], F32, "MB")
    TTs = sb.alloc([128, 8, 640], BF16, "TTs")
    M0t = sb.alloc([128, 512], BF16, "M0t")
    onesb = sb.alloc([128, 64], BF16, "onesb")
    sel = sb.alloc([128, 2], F32, "sel")
    Kwin = [sb.alloc([128, 4, 1536], BF16, "Kwin") for _ in range(2)]
    Vwin = [sb.alloc([128, 12, 512], BF16, "Vwin") for _ in range(2)]
    Ktmp = sb.alloc([128, 4, 1024], BF16, "Ktmp")
    Vtmp = sb.alloc([128, 8, 512], BF16, "Vtmp")
    Ksel = [sb.alloc([128, 4, 1024], BF16, "Ksel") for _ in range(2)]
    Vsel = [sb.alloc([128, 8, 512], BF16, "Vsel") for _ in range(2)]
    Qt = [sb.alloc([128, 4, 512], BF16, "Qt") for _ in range(2)]
    qk = [sb.alloc([128, 512], BF16, "qk") for _ in range(2)]
    sh = [sb.alloc([1, 512], BF16, "sh") for _ in range(2)]
    PT = [sb.alloc([128, 512], BF16, "PT") for _ in range(3)]
    rden = [sb.alloc([64, 512], F32, "rden") for _ in range(2)]
    oS = [sb.alloc([64, 512], BF16, "oS") for _ in range(2)]
    P.dma("sp", TT32[:], K.tbias[l].rearrange("h q c -> q h c"), writes=["TT32"])
    P.dma("sp", MB[:], K.mband, writes=["MB"])
    P.dma("sp", M0t[:], K.m0, writes=["M0t"])
    P.dma("sp", sel[:], K.sel, writes=["sel"])
    zerob = sb.alloc([128, 64], BF16, "zerob")
    P.op("pool", lambda e: e.memset(onesb[:], 1.0), writes=["onesb"])
    P.op("pool", lambda e: e.memset(zerob[:], 0.0), writes=["zerob"])
    for h in range(8):
        P.op("dve", lambda e, h=h: e.tensor_tensor(out=TTs[:, h, :], in0=TT32[:, h, :], in1=MB[:], op=ALU.add),
             reads=["TT32", "MB"], writes=["TTs"])
    gv = GV(K)

    def load_tile(i):
        par = i % 2
        tcols = slice(i * 512, (i + 1) * 512)
        slots = [(1, max(i - 1, 0)), (0, i), (1, i)]
        for s_, (r, ti) in enumerate(slots):
            for jj in range(2):
                P.dma("sp", Kwin[par][:, 2 * jj:2 * jj + 2, s_ * 512:(s_ + 1) * 512],
                      gv.kT("a", r, jj)[:, ti * 512:(ti + 1) * 512].rearrange("(hp p) t -> p hp t", p=128),
                      reads=["gath"], writes=[("Kwin", par, s_, jj)])
            P.dma("sp", Vwin[par][:, s_ * 4:(s_ + 1) * 4, :],
                  gv.va_rows("a", r, ti * 512, 512).rearrange("(b p) f -> p b f", p=128),
                  reads=["gath"], writes=[("Vwin", par, s_)])
        P.dma("sp", Qt[par][:], K.qaT[:, tcols].rearrange("(hp p) t -> p hp t", p=128),
              reads=[("dram", "qa", i)], writes=[("Qt", par)])
        P.op("dve", lambda e, par=par: e.tensor_scalar(out=Ktmp[:], in0=Kwin[par][:, :, 512:1536], scalar1=sel[:, 1:2], scalar2=None, op0=ALU.mult),
             reads=[("Kwin", par, 1, 0), ("Kwin", par, 1, 1), ("Kwin", par, 2, 0), ("Kwin", par, 2, 1), "sel"], writes=["Ktmp"])
        P.op("dve", lambda e, par=par: e.scalar_tensor_tensor(out=Ksel[par][:], in0=Kwin[par][:, :, 0:1024], scalar=sel[:, 0:1], in1=Ktmp[:], op0=ALU.mult, op1=ALU.add),
             reads=[("Kwin", par, 0, 0), ("Kwin", par, 0, 1), ("Kwin", par, 1, 0), ("Kwin", par, 1, 1), "Ktmp", "sel"], writes=[("Ksel", par)])
        P.op("dve", lambda e, par=par: e.tensor_scalar(out=Vtmp[:], in0=Vwin[par][:, 4:12, :], scalar1=sel[:, 1:2], scalar2=None, op0=ALU.mult),
             reads=[("Vwin", par, 1), ("Vwin", par, 2), "sel"], writes=["Vtmp"])
        P.op("dve", lambda e, par=par: e.scalar_tensor_tensor(out=Vsel[par][:], in0=Vwin[par][:, 0:8, :], scalar=sel[:, 0:1], in1=Vtmp[:], op0=ALU.mult, op1=ALU.add),
             reads=[("Vwin", par, 0), ("Vwin", par, 1), "Vtmp", "sel"], writes=[("Vsel", par)])

    heads = [(i, h) for i in range(NT) for h in range(8)]
    steps = [(n, kb) for n in range(len(heads)) for kb in range(8)]
    NBK = 3

    def hinfo(n):
        i, h = heads[n]
        par, hp, hb, op_ = i % 2, h // 2, (h % 2) * 64, n % 2
        num = ps[0:64, (4 + op_) * 512:(5 + op_) * 512]
        den = ps[0:64, (6 + op_) * 512:(7 + op_) * 512]
        return i, h, par, hp, hb, op_, num, den

    def prologue(n):
        i, h, par, hp, hb, op_, num, den = hinfo(n)
        shp = ps[0:1, 3 * 512:4 * 512]
        pq = (n // 2) % 2
        if h % 2 == 0:
            P.op("dve", lambda e: e.tensor_tensor(
                out=qk[pq][:], in0=Qt[par][:, hp, :], in1=Ksel[par][:, hp, 512:1024], op=ALU.mult),
                reads=[("Qt", par), ("Ksel", par)], writes=[("qk", pq)])
        P.op("pe", lambda e: e.matmul(shp, onesb[hb:hb + 64, 0:1], qk[pq][hb:hb + 64, :], start=True, stop=True),
             reads=[("qk", pq), "onesb"], writes=["shp"])
        P.op("act", lambda e: e.activation(out=sh[op_][:], in_=shp, func=AF.Identity, scale=1.0),
             reads=["shp"], writes=[("sh", op_)])

    def prologue2(n):
        i, h, par, hp, hb, op_, num, den = hinfo(n)
        P.op("pe", lambda e: e.matmul(num, zerob[:], Qt[par][:, 0, :], start=True, stop=False),
             reads=["zerob", ("Qt", par)], writes=[("num", op_)])
        P.op("pe", lambda e: e.matmul(den, zerob[:], Qt[par][:, 0, :], start=True, stop=False),
             reads=["zerob", ("Qt", par)], writes=[("den", op_)])

    def geom(kb):
        jlo, jhi = max(0, kb - 4), min(3, kb)
        return jlo, jhi, (jhi - jlo + 1) * 128, jlo * 128

    def score(s):
        n, kb = steps[s]
        i, h, par, hp, hb, op_, num, den = hinfo(n)
        jlo, jhi, ncol, q0 = geom(kb)
        bk = s % NBK
        sT = ps[:, bk * 512:bk * 512 + ncol]
        P.op("pe", lambda e: e.matmul(
            sT, Ksel[par][hb:hb + 64, hp, kb * 128:(kb + 1) * 128], Qt[par][hb:hb + 64, hp, q0:q0 + ncol], start=True, stop=False),
            reads=[("Ksel", par), ("Qt", par)], writes=[("sT", bk)])
        t0 = (4 - kb + jlo) * 128
        P.op("pe", lambda e: e.matmul(sT, K.ident[:], TTs[:, h, t0:t0 + ncol], start=False, stop=False),
             reads=["TTs", "ident"], writes=[("sT", bk)])
        if i == 0 and kb < 4:
            P.op("pe", lambda e: e.matmul(sT, K.ident[:], M0t[:, 0:ncol], start=False, stop=False),
                 reads=["M0t", "ident"], writes=[("sT", bk)])
        P.op("pe", lambda e: e.matmul(sT, K.negones[0:1, 0:128], sh[op_][0:1, q0:q0 + ncol], start=False, stop=True),
             reads=[("sh", op_), "negones"], writes=[("sT", bk)])
        P.op("act", lambda e: e.activation(out=PT[bk][:, 0:ncol], in_=sT, func=AF.Exp, scale=1.0),
             reads=[("sT", bk)], writes=[("PT", bk)])

    def pv(s):
        n, kb = steps[s]
        i, h, par, hp, hb, op_, num, den = hinfo(n)
        jlo, jhi, ncol, q0 = geom(kb)
        bk = s % NBK
        P.op("pe", lambda e: e.matmul(
            num[:, q0:q0 + ncol], Vsel[par][:, kb, h * 64:(h + 1) * 64], PT[bk][:, 0:ncol], start=False, stop=(kb == 7)),
            reads=[("Vsel", par), ("PT", bk)], writes=[("num", op_)])
        P.op("pe", lambda e: e.matmul(
            den[:, q0:q0 + ncol], onesb[:, 0:64], PT[bk][:, 0:ncol], start=False, stop=(kb == 7)),
            reads=["onesb", ("PT", bk)], writes=[("den", op_)])
        if kb == 7:
            P.op("dve", lambda e: e.reciprocal(out=rden[op_][:], in_=den),
                 reads=[("den", op_)], writes=[("rden", op_)])
            P.op("dve", lambda e: e.tensor_tensor(out=oS[op_][:], in0=num, in1=rden[op_][:], op=ALU.mult),
                 reads=[("num", op_), ("rden", op_)], writes=[("oS", op_)])
            P.dma("sp", K.attnT[h * 64:(h + 1) * 64, i * 512:(i + 1) * 512], oS[op_][:],
                  reads=[("oS", op_)], writes=[("dram", "attnA", i, h)])

    LAG = 2
    load_tile(0)
    load_tile(1)
    prologue(0)
    prologue2(0)
    ns = len(steps)
    for s in range(ns + LAG):
        if s < ns:
            n, kb = steps[s]
            i, h = heads[n]
            if kb == 3 and h == 0 and i >= 1 and i + 1 < NT:
                load_tile(i + 1)
            if kb == 2 and n + 1 < len(heads):
                prologue(n + 1)
            if kb == 6 and n + 1 < len(heads):
                prologue2(n + 1)
            score(s)
        if s - LAG >= 0:
            pv(s - LAG)


def phase_attn_b(K, l):
    nc, P, sb, ps = K.nc, K.P, K.sb, K.ps
    sb.reset(K.sb_phase_base)
    KT = [sb.alloc([128, 8192], BF16, "KT") for _ in range(2)]
    Vh = [sb.alloc([128, 64, 128], BF16, "Vh") for _ in range(2)]
    QT = [sb.alloc([128, 4096], BF16, "QT") for _ in range(2)]
    BM = sb.alloc([128, 8, 512], BF16, "BM")
    E = [sb.alloc([128, 1024], F32, "E") for _ in range(3)]
    XC = [sb.alloc([128, 1024], F32, "XC") for _ in range(2)]
    L = [sb.alloc([128, 1024], BF16, "L") for _ in range(2)]
    W = [sb.alloc([128, 1024], BF16, "W") for _ in range(2)]
    tS = sb.alloc([128, 512], BF16, "tS")
    S32 = sb.alloc([128, 512], F32, "S32")
    Sbf = [sb.alloc([128, 512], BF16, "Sbf") for _ in range(2)]
    oS = [sb.alloc([64, 512], BF16, "oSb") for _ in range(2)]
    P.dma("sp", BM[:], K.bmask.rearrange("m b p t -> p (m b) t"), writes=["BM"])
    gv = GV(K)
    zb = ps[:, 0:1024]
    Wb = [ps[:, 1024:2048], ps[:, 2048:3072]]
    ob = [ps[0:64, 3072:3584], ps[0:64, 3584:4096]]

    groups = []
    for hp in range(4):
        for hh in range(2):
            h = hp * 2 + hh
            for i in range(NT):
                tiles = []
                for ti in range(i, -1, -1):
                    tiles.append((1, ti))
                    tiles.append((0, ti))
                seq = []
                for tn, (r, ti) in enumerate(tiles):
                    for half in (1, 0):
                        kc = r * 4096 + ti * 512 + half * 256
                        vb0 = r * 32 + ti * 4 + half * 2
                        mi = (tn * 4 + half * 2) if tn < 2 else None
                        seq.append(dict(hp=hp, hb=hh * 64, h=h, i=i, kc=kc, vb0=vb0, mi=mi))
                seq[0]["first"] = True
                seq[-1]["last"] = True
                groups += seq
    K.ngroups_b = len(groups)

    loaded = set()

    def ensure_loaded(hp):
        if hp in loaded or hp >= 4:
            return
        loaded.add(hp)
        pp = hp % 2
        for r in range(2):
            P.dma("sp", KT[pp][:, r * 4096:(r + 1) * 4096], gv.kT("b", r, hp // 2)[(hp % 2) * 128:(hp % 2 + 1) * 128, :],
                  reads=["gath"], writes=[("KT", pp, r)])
            for q4 in range(4):
                P.dma("sp", Vh[pp][:, r * 32 + q4 * 8:r * 32 + (q4 + 1) * 8, :],
                      gv.va_rows("b", r, q4 * 1024, 1024)[:, hp * 128:(hp + 1) * 128].rearrange("(b p) d -> p b d", p=128),
                      reads=["gath"], writes=[("Vh", pp, r, q4)])
        P.dma("sp", QT[pp][:], K.qbT[hp * 128:(hp + 1) * 128, :],
              reads=[("dram", "qb", i) for i in range(NT)], writes=[("QT", pp)])

    def rd_kv(g):
        pp = g["hp"] % 2
        return [("KT", pp, 0), ("KT", pp, 1), ("QT", pp)]

    def st_z(s):
        g = groups[s]
        pp, hb, i = g["hp"] % 2, g["hb"], g["i"]
        nmm = 2 + (2 if g["mi"] is not None else 0)
        for b in range(2):
            kcol = g["kc"] + (1 - b) * 128
            P.op("pe", lambda e, b=b, kcol=kcol, pp=pp, hb=hb, i=i, g=g: e.matmul(
                zb[:, b * 512:(b + 1) * 512], KT[pp][hb:hb + 64, kcol:kcol + 128], QT[pp][hb:hb + 64, i * 512:(i + 1) * 512],
                start=True, stop=(g["mi"] is None)),
                reads=rd_kv(g), writes=["zb"])
            if g["mi"] is not None:
                mi = g["mi"] + (1 - b)
                P.op("pe", lambda e, b=b, mi=mi: e.matmul(
                    zb[:, b * 512:(b + 1) * 512], K.ident[:], BM[:, mi, :], start=False, stop=True),
                    reads=["BM", "ident"], writes=["zb"])

    def st_E(s):
        q3 = s % 3
        P.op("act", lambda e, q3=q3: e.activation(out=E[q3][:], in_=zb, func=AF.Exp, scale=1.0),
             reads=["zb"], writes=[("E", q3)])

    def st_L(s):
        q = s % 2
        q3 = s % 3
        P.op("act", lambda e, q=q, q3=q3: e.activation(out=L[q][:], in_=E[q3][:], func=AF.Ln, bias=1.0, scale=1.0),
             reads=[("E", q3)], writes=[("L", q)])

    def st_S(s):
        g = groups[s]
        if g.get("last"):
            return
        q = s % 2
        nq = (s + 1) % 2
        P.op("dve", lambda e, q=q: e.tensor_tensor(out=tS[:], in0=L[q][:, 0:512], in1=L[q][:, 512:1024], op=ALU.add),
             reads=[("L", q)], writes=["tS"])
        if g.get("first"):
            P.op("dve", lambda e: e.tensor_copy(out=S32[:], in_=tS[:]), reads=["tS"], writes=["S32"])
        else:
            P.op("dve", lambda e: e.tensor_tensor(out=S32[:], in0=S32[:], in1=tS[:], op=ALU.add), reads=["tS", "S32"], writes=["S32"])
        P.op("pool", lambda e, nq=nq: e.tensor_copy(out=Sbf[nq][:], in_=S32[:]), reads=["S32"], writes=[("Sbf", nq)])

    def st_cum(s):
        g = groups[s]
        q = s % 2
        first = bool(g.get("first"))
        for b in range(2):
            out = Wb[q][:, b * 512:(b + 1) * 512]
            last_here = (first and b == 0)
            P.op("pe", lambda e, out=out, q=q, b=b, last_here=last_here: e.matmul(
                out, K.trineg[:], L[q][:, b * 512:(b + 1) * 512], start=True, stop=last_here),
                reads=[("L", q), "trineg"], writes=[("Wb", q)])
            if b == 1:
                P.op("pe", lambda e, out=out, q=q, first=first: e.matmul(
                    out, K.negones[:], L[q][:, 0:512], start=False, stop=first),
                    reads=[("L", q), "negones"], writes=[("Wb", q)])
            if not first:
                P.op("pe", lambda e, out=out, q=q: e.matmul(out, K.negones[:], Sbf[q][:], start=False, stop=True),
                     reads=[("Sbf", q), "negones"], writes=[("Wb", q)])

    def st_W(s):
        q = s % 2
        q3 = s % 3
        P.op("act", lambda e, q=q: e.activation(out=XC[q][:], in_=Wb[q], func=AF.Exp, scale=1.0),
             reads=[("Wb", q)], writes=[("XC", q)])
        P.op("dve", lambda e, q=q, q3=q3: e.tensor_tensor(out=W[q][:], in0=E[q3][:], in1=XC[q][:], op=ALU.mult),
             reads=[("E", q3), ("XC", q)], writes=[("W", q)])

    def st_wv(s):
        g = groups[s]
        q = s % 2
        pp, hb = g["hp"] % 2, g["hb"]
        opar = (g["h"] * NT + g["i"]) % 2
        for b in range(2):
            vblk = g["vb0"] + (1 - b)
            P.op("pe", lambda e, b=b, vblk=vblk, pp=pp, hb=hb, q=q, opar=opar, g=g: e.matmul(
                ob[opar], Vh[pp][:, vblk, hb:hb + 64], W[q][:, b * 512:(b + 1) * 512],
                start=(bool(g.get("first")) and b == 0), stop=(bool(g.get("last")) and b == 1)),
                reads=[("W", q)] + [("Vh", pp, r, q4) for r in range(2) for q4 in range(4)], writes=[("ob", opar)])
        if g.get("last"):
            h, i = g["h"], g["i"]
            P.op("dve", lambda e, opar=opar: e.tensor_copy(out=oS[opar][:], in_=ob[opar]),
                 reads=[("ob", opar)], writes=[("oSb", opar)])
            P.dma("pool", K.attnT[512 + h * 64:512 + (h + 1) * 64, i * 512:(i + 1) * 512], oS[opar][:],
                  reads=[("oSb", opar)], writes=[("dram", "attnB", i, h)])

    n = len(groups)
    ensure_loaded(0)
    st_z(0)
    st_E(0)
    if n > 1:
        st_z(1)
    for s in range(n):
        g = groups[s]
        if g.get("first") and g["i"] == 0 and g["hb"] == 0:
            ensure_loaded(g["hp"] + 1)
        if s + 1 < n:
            st_E(s + 1)
        if s + 2 < n:
            ensure_loaded(groups[s + 2]["hp"])
            st_z(s + 2)
        st_L(s)
        st_S(s)
        st_cum(s)
        if s >= 1:
            st_W(s - 1)
            st_wv(s - 1)
    st_W(n - 1)
    st_wv(n - 1)


def layer_norm_T(K, r, yout, gcol, bcol, sq, S1, S2, mean, msq, rstd, tmp, keys_r, key_y):
    P = K.P
    tkey = (lambda q: ("sq", q)) if tmp is sq else (lambda q: ("lt", q))
    for c in range(8):
        q = c % 2
        P.op("dve", lambda e, c=c, q=q: e.tensor_tensor(out=sq[q][:], in0=r[:, c, :], in1=r[:, c, :], op=ALU.mult),
             reads=[keys_r(c)], writes=[("sq", q)])
        P.op("pe", lambda e, c=c: e.matmul(S1, K.ones32[:], r[:, c, :], start=(c == 0), stop=(c == 7)),
             reads=[keys_r(c), "ones32"], writes=["S1"])
        P.op("pe", lambda e, c=c, q=q: e.matmul(S2, K.ones32[:], sq[q][:], start=(c == 0), stop=(c == 7)),
             reads=[("sq", q), "ones32"], writes=["S2"])
    P.op("dve", lambda e: e.tensor_scalar(out=mean[:], in0=S1, scalar1=1.0 / D, scalar2=None, op0=ALU.mult), reads=["S1"], writes=["mean"])
    P.op("dve", lambda e: e.tensor_tensor(out=rstd[:], in0=mean[:], in1=mean[:], op=ALU.mult), reads=["mean"], writes=["rstd"])
    P.op("dve", lambda e: e.scalar_tensor_tensor(out=rstd[:], in0=S2, scalar=1.0 / D, in1=rstd[:], op0=ALU.mult, op1=ALU.subtract),
         reads=["S2", "rstd"], writes=["rstd"])
    P.op("dve", lambda e: e.tensor_scalar(out=rstd[:], in0=rstd[:], scalar1=EPS, scalar2=None, op0=ALU.add),
         reads=["rstd"], writes=["rstd"])
    P.op("act", lambda e: e.activation(out=sq[0][:], in_=rstd[:], func=AF.Sqrt, scale=1.0),
         reads=["rstd", "S2"], writes=[("sq", 0)])
    P.op("dve", lambda e: e.reciprocal(out=rstd[:], in_=sq[0][:]),
         reads=[("sq", 0)], writes=["rstd"])
    for c in range(8):
        q = c % 2
        P.op("dve", lambda e, c=c, q=q: e.tensor_tensor(out=tmp[q][:], in0=r[:, c, :], in1=mean[:], op=ALU.subtract),
             reads=[keys_r(c), "mean"], writes=[tkey(q)])
        P.op("dve", lambda e, q=q: e.tensor_tensor(out=tmp[q][:], in0=tmp[q][:], in1=rstd[:], op=ALU.mult),
             reads=[tkey(q), "rstd"], writes=[tkey(q)])
        P.op("act", lambda e, c=c, q=q: e.activation(out=yout[:, c, :], in_=tmp[q][:], func=AF.Identity, bias=bcol[:, c:c + 1], scale=gcol[:, c:c + 1]),
             reads=[tkey(q), "lnp"], writes=[key_y(c)])


def phase_merge(K, l):
    nc, P, sb, ps = K.nc, K.P, K.sb, K.ps
    sb.reset(K.sb_phase_base)
    wpa = sb.alloc([128, 4, D], BF16, "wpa")
    wpb = sb.alloc([128, 4, D], BF16, "wpb")
    wo = sb.alloc([128, 8, D], BF16, "wo")
    stg = [sb.alloc([128, 1024], F32, "stg") for _ in range(2)]
    lng = sb.alloc([128, 8], F32, "lng")
    lnb = sb.alloc([128, 8], F32, "lnb")
    at = [sb.alloc([128, 8, 512], BF16, "at") for _ in range(2)]
    ga = [sb.alloc([128, 8, 512], BF16, "ga") for _ in range(2)]
    gb = [sb.alloc([128, 8, 512], BF16, "gb") for _ in range(2)]
    xf = [sb.alloc([128, 8, 512], F32, "xf") for _ in range(2)]
    t1 = [sb.alloc([128, 512], F32, "t1") for _ in range(2)]
    t2 = [sb.alloc([128, 512], F32, "t2") for _ in range(2)]
    mT = sb.alloc([128, 8, 512], BF16, "mT")
    rr = sb.alloc([128, 8, 512], F32, "rr")
    yo = [sb.alloc([128, 8, 512], F32, "yo") for _ in range(2)]
    sq = [sb.alloc([128, 512], F32, "sq") for _ in range(2)]
    tmp = [sb.alloc([128, 512], F32, "tmp") for _ in range(2)]
    mean = sb.alloc([128, 512], F32, "mean")
    msq = sb.alloc([128, 512], F32, "msq")
    rstd = sb.alloc([128, 512], F32, "rstd")
    xsrc = K.xT if l == 0 else K.x2T
    P.dma("sp", lng[:], K.ln1g[l], writes=["lnp"])
    P.dma("sp", lnb[:], K.ln1b[l], writes=["lnp"])
    load_cast_weight(K, K.w_pa[l], wpa, 4, D, stg, "wpa", 1)
    load_cast_weight(K, K.w_pb[l], wpb, 4, D, stg, "wpb", 1)
    load_cast_weight(K, K.w_out[l], wo, 8, D, stg, "wo", 1)
    S1 = ps[:, 6 * 512:7 * 512]
    S2 = ps[:, 7 * 512:8 * 512]
    for i in range(NT):
        par = i % 2
        tcols = slice(i * 512, (i + 1) * 512)
        P.dma("sp", at[par][:], K.attnT[:, tcols].rearrange("(k p) t -> p k t", p=128),
              reads=[("dram", "attnA", i, h) for h in range(8)] + [("dram", "attnB", i, h) for h in range(8)], writes=[("at", par)])
        P.dma("sp", ga[par][:], K.gaT[:, tcols].rearrange("(k p) t -> p k t", p=128),
              reads=[("dram", "ga0", i), ("dram", "ga1", i)], writes=[("ga", par)])
        P.dma("sp", gb[par][:], K.gbT[:, tcols].rearrange("(k p) t -> p k t", p=128),
              reads=[("dram", "gb0", i), ("dram", "gb1", i)], writes=[("gb", par)])
        P.dma("sp", xf[par][:], xsrc[:, tcols].rearrange("(k p) t -> p k t", p=128), writes=[("xf", par)])
        for c in range(8):
            q = c % 2
            pa = ps[:, (q * 2) * 512:(q * 2 + 1) * 512]
            pb = ps[:, (q * 2 + 1) * 512:(q * 2 + 2) * 512]
            for k in range(4):
                P.op("pe", lambda e, pa=pa, k=k, c=c, par=par: e.matmul(pa, wpa[:, k, c * 128:(c + 1) * 128], at[par][:, k, :], start=(k == 0), stop=(k == 3)),
                     reads=[("wpa", k), ("at", par)], writes=[("pa", q)])
            for k in range(4):
                P.op("pe", lambda e, pb=pb, k=k, c=c, par=par: e.matmul(pb, wpb[:, k, c * 128:(c + 1) * 128], at[par][:, 4 + k, :], start=(k == 0), stop=(k == 3)),
                     reads=[("wpb", k), ("at", par)], writes=[("pb", q)])
            P.op("dve", lambda e, pa=pa, c=c, q=q, par=par: e.tensor_tensor(out=t1[q][:], in0=pa, in1=ga[par][:, c, :], op=ALU.mult),
                 reads=[("pa", q), ("ga", par)], writes=[("t1", q)])
            P.op("dve", lambda e, pb=pb, c=c, q=q, par=par: e.tensor_tensor(out=t2[q][:], in0=pb, in1=gb[par][:, c, :], op=ALU.mult),
                 reads=[("pb", q), ("gb", par)], writes=[("t2", q)])
            P.op("dve", lambda e, c=c, q=q: e.tensor_tensor(out=mT[:, c, :], in0=t1[q][:], in1=t2[q][:], op=ALU.add),
                 reads=[("t1", q), ("t2", q)], writes=[("mT", c)])
        for c in range(8):
            q = c % 2
            pm = ps[:, (4 + q) * 512:(5 + q) * 512]
            for k in range(8):
                P.op("pe", lambda e, pm=pm, k=k, c=c: e.matmul(pm, wo[:, k, c * 128:(c + 1) * 128], mT[:, k, :], start=(k == 0), stop=(k == 7)),
                     reads=[("wo", k), ("mT", k)], writes=[("pm", q)])
            P.op("dve", lambda e, pm=pm, c=c, par=par: e.scalar_tensor_tensor(out=rr[:, c, :], in0=xf[par][:, c, :], scalar=ALPHA, in1=pm, op0=ALU.mult, op1=ALU.add),
                 reads=[("pm", q), ("xf", par)], writes=[("rr", c)])
        layer_norm_T(K, rr, yo[par], lng, lnb, sq, S1, S2, mean, msq, rstd, tmp,
                     keys_r=lambda c: ("rr", c), key_y=lambda c, par=par: ("yo", par, c))
        P.dma("pool", K.x1T[:, tcols].rearrange("(k p) t -> p k t", p=128), yo[par][:],
              reads=[("yo", par, c) for c in range(8)], writes=[("dram", "x1", i)])


def phase_ffn(K, l, last):
    nc, P, sb, ps = K.nc, K.P, K.sb, K.ps
    sb.reset(K.sb_phase_base)
    wf1 = sb.alloc([128, 8, 2 * DFF], BF16, "wf1")
    wf2 = sb.alloc([128, 22, D], BF16, "wf2")
    lng = sb.alloc([128, 8], F32, "lng")
    lnb = sb.alloc([128, 8], F32, "lnb")
    xfs = [sb.alloc([128, 8, 512], F32, "xf") for _ in range(2)]
    xb = sb.alloc([128, 8, 512], BF16, "xb")
    hT = sb.alloc([128, 22, 512], BF16, "hT")
    sg = [sb.alloc([128, 512], F32, "sg") for _ in range(2)]
    sq = [sb.alloc([128, 512], F32, "sq") for _ in range(2)]
    tmp = sq
    mean = sb.alloc([128, 512], F32, "mean")
    msq = None
    rstd = sb.alloc([128, 512], F32, "rstd")
    P.dma("sp", lng[:], K.ln2g[l], writes=["lnp"])
    P.dma("sp", lnb[:], K.ln2b[l], writes=["lnp"])

    def load_xf(i):
        P.dma("sp", xfs[i % 2][:], K.x1T[:, i * 512:(i + 1) * 512].rearrange("(k p) t -> p k t", p=128),
              reads=[("dram", "x1", i)], writes=[("xf", i % 2, c) for c in range(8)])

    load_xf(0)
    load_cast_weight(K, K.w_f1[l], wf1, 8, 2 * DFF, None, "wf1", 8)
    load_cast_weight(K, K.w_f2[l], wf2, 22, D, None, "wf2", 2)
    S1 = ps[:, 6 * 512:7 * 512]
    S2 = ps[:, 7 * 512:8 * 512]
    dst = K.outT if last else K.x2T
    outs = []
    for i in range(NT):
        par = i % 2
        xf = xfs[par]
        tcols = slice(i * 512, (i + 1) * 512)
        if i + 1 < NT:
            load_xf(i + 1)
        P.op("dve", lambda e, xf=xf: e.tensor_copy(out=xb[:], in_=xf[:]), reads=[("xf", par, c) for c in range(8)], writes=["xb"])
        for j in range(22):
            q = j % 2
            pg = ps[:, (q * 2) * 512:(q * 2 + 1) * 512]
            pu = ps[:, (q * 2 + 1) * 512:(q * 2 + 2) * 512]
            cg = j * 128
            cu = DFF + j * 128
            for k in range(8):
                P.op("pe", lambda e, pg=pg, k=k, cg=cg: e.matmul(pg, wf1[:, k, cg:cg + 128], xb[:, k, :], start=(k == 0), stop=(k == 7)),
                     reads=[("wf1", k), "xb"], writes=[("pg", q)])
            for k in range(8):
                P.op("pe", lambda e, pu=pu, k=k, cu=cu: e.matmul(pu, wf1[:, k, cu:cu + 128], xb[:, k, :], start=(k == 0), stop=(k == 7)),
                     reads=[("wf1", k), "xb"], writes=[("pu", q)])
            P.op("act", lambda e, pg=pg, q=q: e.activation(out=sg[q][:], in_=pg, func=AF.Silu, scale=1.0),
                 reads=[("pg", q)], writes=[("sg", q)])
            P.op("dve", lambda e, pu=pu, q=q, j=j: e.tensor_tensor(out=hT[:, j, :], in0=pu, in1=sg[q][:], op=ALU.mult),
                 reads=[("pu", q), ("sg", q)], writes=[("hT", j)])
        for c in range(8):
            q = c % 2
            pm = ps[:, (4 + q) * 512:(5 + q) * 512]
            for j in range(22):
                P.op("pe", lambda e, pm=pm, j=j, c=c: e.matmul(pm, wf2[:, j, c * 128:(c + 1) * 128], hT[:, j, :], start=(j == 0), stop=(j == 21)),
                     reads=[("wf2", j), ("hT", j)], writes=[("pm", q)])
            P.op("dve", lambda e, pm=pm, c=c, xf=xf: e.scalar_tensor_tensor(out=xf[:, c, :], in0=xf[:, c, :], scalar=ALPHA, in1=pm, op0=ALU.mult, op1=ALU.add),
                 reads=[("pm", q), ("xf", par, c)], writes=[("xf", par, c)])
        layer_norm_T(K, xf, xf, lng, lnb, sq, S1, S2, mean, msq, rstd, tmp,
                     keys_r=lambda c, par=par: ("xf", par, c), key_y=lambda c, par=par: ("xf", par, c))
        o = P.dma("sp", dst[:, tcols].rearrange("(k p) t -> p k t", p=128), xf[:],
                  reads=[("xf", par, c) for c in range(8)], writes=[("dram", "x2", i)])
        outs.append(o)
    return outs


def build(debug=False, stop_after=None):
    nc = bass.Bass("TRN2", target_bir_lowering=False)
    K = Ctx()
    K.nc = nc
    K.P = Prog()
    K.sb = SbAlloc(nc)
    K.stgn = 0

    def din(name, shape, dt=F32):
        return nc.dram_tensor(name, list(shape), dt, kind="ExternalInput").ap()

    def dscr(name, shape, dt):
        kind = "ExternalOutput" if (debug and name in debug) else "Internal"
        return nc.dram_tensor(name, list(shape), dt, kind=kind)

    K.xT = din("xT", [D, T])
    K.w_in = din("w_in", [DEPTH, D, INC])
    K.w_pa = din("w_pa", [DEPTH, 512, D])
    K.w_pb = din("w_pb", [DEPTH, 512, D])
    K.w_out = din("w_out", [DEPTH, D, D])
    K.w_f1 = din("w_f1", [DEPTH, D, 2 * DFF])
    K.w_f2 = din("w_f2", [DEPTH, DFF, D])
    K.bgate = din("bgate", [DEPTH, 128, 16])
    K.ln1g = din("ln1g", [DEPTH, 128, 8])
    K.ln1b = din("ln1b", [DEPTH, 128, 8])
    K.ln2g = din("ln2g", [DEPTH, 128, 8])
    K.ln2b = din("ln2b", [DEPTH, 128, 8])
    K.tbias = din("tbias", [DEPTH, 8, 128, 640])
    K.mband = din("mband", [128, 640])
    K.m0 = din("m0", [128, 512], BF16)
    K.sel = din("sel", [128, 2])
    K.bmask = din("bmask", [2, 4, 128, 512], BF16)
    identd = din("identd", [128, 128], BF16)
    trinegd = din("trinegd", [128, 128], BF16)
    K.outT = nc.dram_tensor("outT", [D, T], F32, kind="ExternalOutput").ap()

    K.qaT = dscr("qaT", [512, T], BF16).ap()
    K.qbT = dscr("qbT", [512, T], BF16).ap()
    K.gaT = dscr("gaT", [D, T], BF16).ap()
    K.gbT = dscr("gbT", [D, T], BF16).ap()
    K.attnT = dscr("attnT", [D, T], BF16).ap()
    K.x1T = dscr("x1T", [D, T], F32).ap()
    K.x2T = dscr("x2T", [D, T], F32).ap()
    cten = {}
    gten = {}
    for l in range(DEPTH):
        for kind in ("va", "vb", "ka", "kb"):
            for j in range(2):
                cten[(l, kind, j)] = nc.dram_tensor("c_%s%d_%d" % (kind, j, l), [2048, 512], BF16)
                gten[(l, kind, j)] = nc.dram_tensor("g_%s%d_%d" % (kind, j, l), [4096, 512], BF16)

    K.ps = nc.alloc_psum_tensor("ps", [128, 4096], F32)
    K.ident = K.sb.alloc([128, 128], BF16, "ident")
    K.trineg = K.sb.alloc([128, 128], BF16, "trineg")
    K.negones = K.sb.alloc([128, 128], BF16, "negones")
    K.ones32 = K.sb.alloc([128, 128], F32, "ones32")
    K.sb_phase_base = K.sb.off
    P = K.P
    P.dma("sp", K.ident[:], identd, writes=["ident"])
    P.dma("sp", K.trineg[:], trinegd, writes=["trineg"])
    P.op("pool", lambda e: e.memset(K.negones[:], -1.0), writes=["negones"])
    P.op("pool", lambda e: e.memset(K.ones32[:], 1.0), writes=["ones32"])

    def persist():
        pass

    outs = []
    done = False
    for l in range(DEPTH):
        K.c_va = [cten[(l, "va", j)].ap() for j in range(2)]
        K.c_vb = [cten[(l, "vb", j)].ap() for j in range(2)]
        K.c_kaT = [cten[(l, "ka", j)].ap().rearrange("(f a) b -> f (a b)", f=256) for j in range(2)]
        K.c_kbT = [cten[(l, "kb", j)].ap().rearrange("(f a) b -> f (a b)", f=256) for j in range(2)]
        K.g_va = [gten[(l, "va", j)].ap() for j in range(2)]
        K.g_vb = [gten[(l, "vb", j)].ap() for j in range(2)]
        K.g_kaT = [gten[(l, "ka", j)].ap() for j in range(2)]
        K.g_kbT = [gten[(l, "kb", j)].ap() for j in range(2)]
        K.cc_list = []
        for kind in ("va", "vb"):
            for j in range(2):
                K.cc_list.append((cten[(l, kind, j)], gten[(l, kind, j)], [("dram", kind, i) for i in range(4 * j, 4 * j + 4)]))
        for kind in ("ka", "kb"):
            for j in range(2):
                K.cc_list.append((cten[(l, kind, j)], gten[(l, kind, j)], [("dram", kind, i, j) for i in range(NT)]))
        for name, fn in (("proj", lambda: phase_proj(K, l)),
                         ("attn_a", lambda: phase_attn_a(K, l)),
                         ("attn_b", lambda: phase_attn_b(K, l)),
                         ("merge", lambda: phase_merge(K, l)),
                         ("ffn", lambda: phase_ffn(K, l, l == DEPTH - 1))):
            r = fn()
            if name == "ffn":
                outs = r
            P.barrier()
            if stop_after == (l, name):
                done = True
                break
        if done:
            break
    P.wait_all("sp", [])
    P.emit(nc)
    K.nc = nc
    return nc, K


def host_inputs(x, w_in, b_gate, rel_bias, w_proj_a, w_proj_b, w_out, ln1_g, ln1_b, w_ffn_in, w_ffn_out, ln2_g, ln2_b):
    f32 = np.float32
    common = {
        "w_in": np.ascontiguousarray(w_in, f32), "w_pa": np.ascontiguousarray(w_proj_a, f32),
        "w_pb": np.ascontiguousarray(w_proj_b, f32), "w_out": np.ascontiguousarray(w_out, f32),
        "w_f1": np.ascontiguousarray(w_ffn_in, f32), "w_f2": np.ascontiguousarray(w_ffn_out, f32),
        "bgate": np.ascontiguousarray(np.asarray(b_gate, f32).reshape(DEPTH, 16, 128).transpose(0, 2, 1)),
        "ln1g": np.ascontiguousarray(np.asarray(ln1_g, f32).reshape(DEPTH, 8, 128).transpose(0, 2, 1)),
        "ln1b": np.ascontiguousarray(np.asarray(ln1_b, f32).reshape(DEPTH, 8, 128).transpose(0, 2, 1)),
        "ln2g": np.ascontiguousarray(np.asarray(ln2_g, f32).reshape(DEPTH, 8, 128).transpose(0, 2, 1)),
        "ln2b": np.ascontiguousarray(np.asarray(ln2_b, f32).reshape(DEPTH, 8, 128).transpose(0, 2, 1)),
    }
    p_ = np.arange(128)[:, None]
    col = np.arange(640)[None, :]
    sig = col // 128
    qq = col % 128
    delta = 4 - sig
    idx = np.clip(512 - 128 * delta + qq - p_, -256, 256) + 256
    common["tbias"] = np.ascontiguousarray(np.asarray(rel_bias, f32)[:, :, idx])
    dchunk = 8 - 2 * delta + qq // 64 - p_ // 64
    valid = (dchunk >= 0) & (dchunk <= 8)
    common["mband"] = np.where(valid, 0.0, NEG).astype(f32)
    common["identd"] = np.eye(128, dtype=f32).astype(ml_dtypes.bfloat16)
    jj = np.arange(128)[:, None]
    ss = np.arange(128)[None, :]
    common["trinegd"] = np.where(jj >= ss, -1.0, 0.0).astype(f32).astype(ml_dtypes.bfloat16)
    kp = (np.arange(4)[:, None, None] * 128 + np.arange(128)[None, :, None])
    tq = np.arange(512)[None, None, :]
    causal = np.where(kp >= tq, NEG, 0.0).astype(f32)
    full = np.full((4, 128, 512), NEG, f32)
    none = np.zeros((4, 128, 512), f32)
    bm = [np.stack([full, causal]), np.stack([causal, none])]
    m0 = [np.full((128, 512), NEG, f32).astype(ml_dtypes.bfloat16), np.zeros((128, 512), f32).astype(ml_dtypes.bfloat16)]
    sel = [np.tile(np.array([[1.0, 0.0]], f32), (128, 1)), np.tile(np.array([[0.0, 1.0]], f32), (128, 1))]
    in_maps = []
    x = np.asarray(x, f32)
    for core in range(8):
        b, r = core // 2, core % 2
        xt = x[b].reshape(16, 512, D)[r::2].reshape(T, D)
        m = dict(common)
        m["xT"] = np.ascontiguousarray(xt.T)
        m["bmask"] = bm[r].astype(ml_dtypes.bfloat16)
        m["m0"] = m0[r]
        m["sel"] = sel[r]
        in_maps.append(m)
    return in_maps


_CACHE = {}


def kernel(**inputs):
    in_maps = host_inputs(**inputs)
    if "nc" not in _CACHE:
        _CACHE["nc"] = build()[0]
    nc = _CACHE["nc"]
    res = run_bass_kernel_spmd(nc, in_maps, core_ids=list(range(8)))
    out = np.empty((NB, S, D), np.float32)
    for core in range(8):
        b, r = core // 2, core % 2
        o = np.asarray(res.results[core]["outT"], np.float32).T.reshape(8, 512, D)
        out[b].reshape(16, 512, D)[r::2] = o
    return out
```

```python
import contextlib
import numpy as np
import ml_dtypes
import concourse.bass as bass
import concourse.mybir as mybir
from concourse.bass_utils import run_bass_kernel_spmd

F32 = mybir.dt.float32
BF16 = mybir.dt.bfloat16
AF = mybir.ActivationFunctionType
ALU = mybir.AluOpType
AX = mybir.AxisListType

D = 1024
S = 8192
NB = 4
DEPTH = 2
DFF = 2816
INC = 5120
T = 4096
NT = 8
ALPHA = float((2 * DEPTH) ** 0.25)
EPS = 1e-5
NEG = -30000.0

COMPUTE = ("pe", "act", "dve", "pool")
QUEUES = ("sp", "pe", "act", "dve", "pool")
NDMA_HW = 12
NDMA_SW = 8
NDMASEM = NDMA_HW + NDMA_SW


class Op:
    __slots__ = ("eng", "fn", "deps", "idx", "signal", "semid", "count", "ndma", "barrier", "cc")

    def __init__(self, eng, fn):
        self.eng = eng
        self.fn = fn
        self.deps = set()
        self.signal = False
        self.semid = None
        self.count = 0
        self.ndma = 0
        self.barrier = False
        self.cc = False


class Prog:
    def __init__(self):
        self.ops = []
        self.lastw = {}
        self.readers = {}

    def op(self, eng, fn, reads=(), writes=(), ndma=0, cc=False):
        o = Op(eng, fn)
        o.idx = len(self.ops)
        o.ndma = ndma
        o.cc = cc
        for k in reads:
            w = self.lastw.get(k)
            if w is not None:
                o.deps.add(w)
        for k in writes:
            w = self.lastw.get(k)
            if w is not None:
                o.deps.add(w)
            for r in self.readers.get(k, ()):
                o.deps.add(r)
        for k in reads:
            lst = self.readers.setdefault(k, [])
            if not ndma and not cc:
                lst[:] = [r for r in lst if not (self.ops[r].eng == eng and not self.ops[r].ndma and not self.ops[r].cc)]
            lst.append(o.idx)
        for k in writes:
            self.lastw[k] = o.idx
            self.readers[k] = []
        o.deps.discard(o.idx)
        self.ops.append(o)
        return o

    def barrier(self):
        o = Op("barrier", None)
        o.idx = len(self.ops)
        o.barrier = True
        self.ops.append(o)
        self.lastw = {}
        self.readers = {}

    def dma(self, q, out, in_, reads=(), writes=()):
        return self.op(q, lambda e: [e.dma_start(out=out, in_=in_)], reads, writes, ndma=1)

    def wait_all(self, q, ops_):
        o = Op(q, None)
        o.idx = len(self.ops)
        o.deps = {x.idx for x in ops_}
        self.ops.append(o)
        return o

    def emit(self, nc):
        ops = self.ops
        slot_last = [None] * NDMASEM
        rr = {"hw": 0, "sw": 0}
        for o in ops:
            if o.barrier or not o.ndma:
                continue
            if o.eng == "pool":
                sl = NDMA_HW + rr["sw"]
                rr["sw"] = (rr["sw"] + 1) % NDMA_SW
            else:
                sl = rr["hw"]
                rr["hw"] = (rr["hw"] + 1) % NDMA_HW
            o.semid = sl
            if slot_last[sl] is not None:
                o.deps.add(slot_last[sl])
            slot_last[sl] = o.idx
        last_c = {}
        last_d = [None] * NDMASEM
        last_cc = []
        pending = {q: set() for q in QUEUES}
        for o in ops:
            if o.barrier:
                snap = set(last_c.values()) | {x for x in last_d if x is not None} | set(last_cc)
                for q in QUEUES:
                    pending[q] |= snap
                continue
            if pending[o.eng]:
                o.deps |= pending[o.eng]
                pending[o.eng] = set()
            if o.ndma:
                last_d[o.semid] = o.idx
            elif o.cc:
                last_cc.append(o.idx)
            elif o.fn is not None:
                last_c[o.eng] = o.idx
        for o in ops:
            if o.barrier:
                continue
            if o.eng == "pe" and not o.ndma:
                o.deps = {d for d in o.deps if not (ops[d].eng == "pe" and not ops[d].ndma)}
        for o in ops:
            if o.barrier:
                continue
            for d in o.deps:
                ops[d].signal = True
        cnt = {e: 0 for e in COMPUTE}
        dcnt = [0] * NDMASEM
        ncc = 0
        for o in ops:
            if o.barrier or o.fn is None:
                continue
            if o.ndma:
                dcnt[o.semid] += 16 * o.ndma
                o.count = dcnt[o.semid]
            elif o.cc:
                o.semid = ncc
                ncc += 1
                o.count = 1
            elif o.signal:
                cnt[o.eng] += 1
                o.count = cnt[o.eng]
        self.final_counts = (dict(cnt), list(dcnt))
        nwaits = [0]
        with contextlib.ExitStack() as st:
            sems = {e: st.enter_context(nc.semaphore("s_" + e)) for e in COMPUTE}
            dsems = [st.enter_context(nc.semaphore("d_%d" % i)) for i in range(NDMASEM)]
            csems = [st.enter_context(nc.semaphore("c_%d" % i)) for i in range(ncc)]
            block = st.enter_context(nc.Block())

            def run_engine(engname, e):
                waited = {}
                for o in ops:
                    if o.barrier or o.eng != engname:
                        continue
                    need = {}
                    for d in o.deps:
                        p = ops[d]
                        if p.ndma:
                            key = ("d", p.semid)
                        elif p.cc:
                            key = ("cc", p.semid)
                        else:
                            key = ("c", p.eng)
                        assert p.count > 0, (p.eng, p.idx)
                        if p.count > need.get(key, 0):
                            need[key] = p.count
                    for key, v in need.items():
                        if waited.get(key, 0) >= v:
                            continue
                        waited[key] = v
                        if key[0] == "d":
                            sem = dsems[key[1]]
                        elif key[0] == "cc":
                            sem = csems[key[1]]
                        else:
                            sem = sems[key[1]]
                        e.wait_ge(sem, v)
                        nwaits[0] += 1
                    if o.fn is None:
                        continue
                    res = o.fn(e)
                    if o.ndma:
                        assert len(res) == o.ndma, (len(res), o.ndma)
                        for ins in res:
                            ins.then_inc(dsems[o.semid], 16)
                    elif o.cc:
                        res.then_inc(csems[o.semid], 1)
                    elif o.signal:
                        res.then_inc(sems[o.eng], 1)

            @block.sync
            def _(e):
                run_engine("sp", e)

            @block.tensor
            def _(e):
                run_engine("pe", e)

            @block.scalar
            def _(e):
                run_engine("act", e)

            @block.vector
            def _(e):
                run_engine("dve", e)

            @block.gpsimd
            def _(e):
                run_engine("pool", e)
        self.nwaits = nwaits[0]


class SbAlloc:
    def __init__(self, nc, base=16640, limit=229000):
        self.nc = nc
        self.base = base
        self.off = base
        self.limit = limit
        self.n = 0

    def reset(self, to=None):
        self.off = self.base if to is None else to

    def alloc(self, shape, dtype, name=None):
        esz = 4 if dtype == F32 else 2
        nbytes = esz * int(np.prod(shape[1:]))
        nbytes = (nbytes + 63) // 64 * 64
        assert self.off + nbytes <= self.limit, ("SBUF overflow", name, self.off, nbytes)
        self.n += 1
        t = self.nc.alloc_sbuf_tensor_at("%s_%d" % (name or "t", self.n), list(shape), dtype, offset=self.off)
        self.off += nbytes
        return t


class Ctx:
    pass


def load_cast_weight(K, src2d, dst, nk, ncols, stg, keyname, split):
    P = K.P
    for k in range(nk):
        P.dma("pool", dst[:, k, :], src2d[k * 128:(k + 1) * 128, :], writes=[(keyname, k)])


def phase_proj(K, l):
    import os
    nc, P, sb, ps = K.nc, K.P, K.sb, K.ps
    sb.reset(K.sb_phase_base)
    wbf = sb.alloc([128, 8, INC], BF16, "wbf")
    bg = sb.alloc([128, 16], F32, "bg")
    xf = [sb.alloc([128, 8, 512], F32, "xf") for _ in range(2)]
    xb = [sb.alloc([128, 8, 512], BF16, "xb") for _ in range(NT)]
    ofm = [sb.alloc([128, 4, 512], BF16, "ofm") for _ in range(3)]
    otm = [sb.alloc([128, 4, 512], BF16, "otm") for _ in range(2)]
    xsrc = K.xT if l == 0 else K.x2T
    P.dma("sp", bg[:], K.bgate[l], writes=["bg"])

    def load_x(i):
        P.dma("sp", xf[i % 2][:], xsrc[:, i * 512:(i + 1) * 512].rearrange("(k p) t -> p k t", p=128), writes=[("xf", i % 2)])
        P.op("dve", lambda e, i=i: e.tensor_copy(out=xb[i][:], in_=xf[i % 2][:]),
             reads=[("xf", i % 2)], writes=[("xb", i)])

    load_x(0)
    load_cast_weight(K, K.w_in[l], wbf, 8, INC, None, "wbf", 2)
    st = dict(nev=0, bank=0, og=0)

    def fm_group(i, col0, kind):
        tcols = slice(i * 512, (i + 1) * 512)
        ob = st["og"] % 3
        st["og"] += 1
        for c4 in range(4):
            bank = st["bank"] % 4
            st["bank"] += 1
            pst = ps[:, bank * 512:(bank + 1) * 512]
            c0 = col0 + c4 * 128
            for k in range(8):
                P.op("pe", lambda e, pst=pst, k=k, c0=c0, i=i: e.matmul(
                    pst, wbf[:, k, c0:c0 + 128], xb[i][:, k, :], start=(k == 0), stop=(k == 7)),
                    reads=[("wbf", k), ("xb", i)], writes=[("ps", bank)])
            dst = ofm[ob][:, c4, :]
            if kind in ("qa", "qb", "ka", "kb"):
                sc = 0.125 if kind[0] == "q" else 1.0
                if st["nev"] % 2 == 0:
                    P.op("act", lambda e, dst=dst, pst=pst, sc=sc: e.activation(out=dst, in_=pst, func=AF.Identity, scale=sc),
                         reads=[("ps", bank)], writes=[("ofm", ob, c4)])
                else:
                    P.op("dve", lambda e, dst=dst, pst=pst, sc=sc: e.tensor_scalar(out=dst, in0=pst, scalar1=sc, scalar2=None, op0=ALU.mult),
                         reads=[("ps", bank)], writes=[("ofm", ob, c4)])
                st["nev"] += 1
            else:
                bidx = (0 if kind[1] == "a" else 8) + int(kind[2]) * 4 + c4
                P.op("act", lambda e, dst=dst, pst=pst, bidx=bidx: e.activation(out=dst, in_=pst, func=AF.Sigmoid, bias=bg[:, bidx:bidx + 1], scale=1.0),
                     reads=[("ps", bank), "bg"], writes=[("ofm", ob, c4)])
        if kind in ("ka", "kb"):
            cl = K.c_kaT if kind == "ka" else K.c_kbT
            for jj in range(2):
                P.dma("sp", cl[jj][:, tcols].rearrange("(c p) t -> p c t", p=128), ofm[ob][:, 2 * jj:2 * jj + 2, :],
                      reads=[("ofm", ob, c4) for c4 in range(4)], writes=[("dram", kind, i, jj)])
            return
        if kind == "qa":
            dd = K.qaT[:, tcols]
        elif kind == "qb":
            dd = K.qbT[:, tcols]
        elif kind[0:2] == "ga":
            r0 = int(kind[2]) * 512
            dd = K.gaT[r0:r0 + 512, tcols]
        else:
            r0 = int(kind[2]) * 512
            dd = K.gbT[r0:r0 + 512, tcols]
        P.dma("sp", dd.rearrange("(c p) t -> p c t", p=128), ofm[ob][:],
              reads=[("ofm", ob, c4) for c4 in range(4)], writes=[("dram", kind, i)])

    def tm_v(i):
        for wi, (col0, kind) in enumerate([(1024, "va"), (2560, "vb")]):
            for sub in range(4):
                bank = 4 + (st["bank"] % 4)
                st["bank"] += 1
                pst = ps[:, bank * 512:(bank + 1) * 512]
                for k in range(8):
                    P.op("pe", lambda e, pst=pst, k=k, col0=col0, i=i, sub=sub: e.matmul(
                        pst, xb[i][:, k, sub * 128:(sub + 1) * 128], wbf[:, k, col0:col0 + 512], start=(k == 0), stop=(k == 7)),
                        reads=[("wbf", k), ("xb", i)], writes=[("ps", bank)])
                dst = otm[wi][:, sub, :]
                if st["nev"] % 2 == 0:
                    P.op("act", lambda e, dst=dst, pst=pst: e.activation(out=dst, in_=pst, func=AF.Identity, scale=1.0),
                         reads=[("ps", bank)], writes=[("otm", wi, sub)])
                else:
                    P.op("dve", lambda e, dst=dst, pst=pst: e.tensor_copy(out=dst, in_=pst),
                         reads=[("ps", bank)], writes=[("otm", wi, sub)])
                st["nev"] += 1
            cv = (K.c_va if kind == "va" else K.c_vb)[i // 4]
            P.dma("sp", cv[(i % 4) * 512:(i % 4 + 1) * 512, :].rearrange("(s p) f -> p s f", p=128), otm[wi][:],
                  reads=[("otm", wi, sub) for sub in range(4)], writes=[("dram", kind, i)])

    for i in range(NT):
        if i + 1 < NT:
            load_x(i + 1)
        for col0, kind in [(0, "qa"), (512, "ka"), (1536, "qb"), (2048, "kb")]:
            fm_group(i, col0, kind)
        tm_v(i)
    if not os.environ.get("KSKIPCC"):
        for (cten, gten, keys) in K.cc_list:
            P.op("pool", lambda e, cten=cten, gten=gten: e.collective_compute(
                "AllGather", ALU.bypass, replica_groups=[[0, 1], [2, 3], [4, 5], [6, 7]],
                ins=[cten.ap().opt()], outs=[gten.ap().opt()]),
                reads=keys, writes=["gath"], cc=True)
    for i in range(NT):
        for col0, kind in [(3072, "ga0"), (3584, "ga1"), (4096, "gb0"), (4608, "gb1")]:
            fm_group(i, col0, kind)


class GV:
    def __init__(self, K):
        self.K = K

    def va_rows(self, which, r, t0, n):
        g = (self.K.g_va if which == "a" else self.K.g_vb)[t0 // 2048]
        o = r * 2048 + t0 % 2048
        return g[o:o + n, :]

    def kT(self, which, r, j):
        g = (self.K.g_kaT if which == "a" else self.K.g_kbT)[j]
        return g[r * 2048:(r + 1) * 2048, :].rearrange("(f a) b -> f (a b)", f=256)


def phase_attn_a(K, l):
    nc, P, sb, ps = K.nc, K.P, K.sb, K.ps
    sb.reset(K.sb_phase_base)
    TT32 = sb.alloc([128, 8, 640], F32, "TT32")
    MB = sb.alloc([128, 640], F32, "MB")
    TTs = sb.alloc([128, 8, 640], BF16, "TTs")
    M0t = sb.alloc([128, 512], BF16, "M0t")
    onesb = sb.alloc([128, 64], BF16, "onesb")
    sel = sb.alloc([128, 2], F32, "sel")
    Kwin = [sb.alloc([128, 4, 1536], BF16, "Kwin") for _ in range(2)]
    Vwin = [sb.alloc([128, 12, 512], BF16, "Vwin") for _ in range(2)]
    Ktmp = sb.alloc([128, 4, 1024], BF16, "Ktmp")
    Vtmp = sb.alloc([128, 8, 512], BF16, "Vtmp")
    Ksel = [sb.alloc([128, 4, 1024], BF16, "Ksel") for _ in range(2)]
    Vsel = [sb.alloc([128, 8, 512], BF16, "Vsel") for _ in range(2)]
    Qt = [sb.alloc([128, 4, 512], BF16, "Qt") for _ in range(2)]
    qk = [sb.alloc([128, 512], BF16, "qk") for _ in range(2)]
    sh = [sb.alloc([1, 512], BF16, "sh") for _ in range(2)]
    PT = [sb.alloc([128, 512], BF16, "PT") for _ in range(3)]
    rden = [sb.alloc([64, 512], F32, "rden") for _ in range(2)]
    oS = [sb.alloc([64, 512], BF16, "oS") for _ in range(2)]
    P.dma("sp", TT32[:], K.tbias[l].rearrange("h q c -> q h c"), writes=["TT32"])
    P.dma("sp", MB[:], K.mband, writes=["MB"])
    P.dma("sp", M0t[:], K.m0, writes=["M0t"])
    P.dma("sp", sel[:], K.sel, writes=["sel"])
    zerob = sb.alloc([128, 64], BF16, "zerob")
    P.op("pool", lambda e: e.memset(onesb[:], 1.0), writes=["onesb"])
    P.op("pool", lambda e: e.memset(zerob[:], 0.0), writes=["zerob"])
    for h in range(8):
        P.op("dve", lambda e, h=h: e.tensor_tensor(out=TTs[:, h, :], in0=TT32[:, h, :], in1=MB[:], op=ALU.add),
             reads=["TT32", "MB"], writes=["TTs"])
    gv = GV(K)

    def load_tile(i):
        par = i % 2
        tcols = slice(i * 512, (i + 1) * 512)
        slots = [(1, max(i - 1, 0)), (0, i), (1, i)]
        for s_, (r, ti) in enumerate(slots):
            for jj in range(2):
                P.dma("sp", Kwin[par][:, 2 * jj:2 * jj + 2, s_ * 512:(s_ + 1) * 512],
                      gv.kT("a", r, jj)[:, ti * 512:(ti + 1) * 512].rearrange("(hp p) t -> p hp t", p=128),
                      reads=["gath"], writes=[("Kwin", par, s_, jj)])
            P.dma("sp", Vwin[par][:, s_ * 4:(s_ + 1) * 4, :],
                  gv.va_rows("a", r, ti * 512, 512).rearrange("(b p) f -> p b f", p=128),
                  reads=["gath"], writes=[("Vwin", par, s_)])
        P.dma("sp", Qt[par][:], K.qaT[:, tcols].rearrange("(hp p) t -> p hp t", p=128),
              reads=[("dram", "qa", i)], writes=[("Qt", par)])
        P.op("dve", lambda e, par=par: e.tensor_scalar(out=Ktmp[:], in0=Kwin[par][:, :, 512:1536], scalar1=sel[:, 1:2], scalar2=None, op0=ALU.mult),
             reads=[("Kwin", par, 1, 0), ("Kwin", par, 1, 1), ("Kwin", par, 2, 0), ("Kwin", par, 2, 1), "sel"], writes=["Ktmp"])
        P.op("dve", lambda e, par=par: e.scalar_tensor_tensor(out=Ksel[par][:], in0=Kwin[par][:, :, 0:1024], scalar=sel[:, 0:1], in1=Ktmp[:], op0=ALU.mult, op1=ALU.add),
             reads=[("Kwin", par, 0, 0), ("Kwin", par, 0, 1), ("Kwin", par, 1, 0), ("Kwin", par, 1, 1), "Ktmp", "sel"], writes=[("Ksel", par)])
        P.op("dve", lambda e, par=par: e.tensor_scalar(out=Vtmp[:], in0=Vwin[par][:, 4:12, :], scalar1=sel[:, 1:2], scalar2=None, op0=ALU.mult),
             reads=[("Vwin", par, 1), ("Vwin", par, 2), "sel"], writes=["Vtmp"])
        P.op("dve", lambda e, par=par: e.scalar_tensor_tensor(out=Vsel[par][:], in0=Vwin[par][:, 0:8, :], scalar=sel[:, 0:1], in1=Vtmp[:], op0=ALU.mult, op1=ALU.add),
             reads=[("Vwin", par, 0), ("Vwin", par, 1), "Vtmp", "sel"], writes=[("Vsel", par)])

    heads = [(i, h) for i in range(NT) for h in range(8)]
    steps = [(n, kb) for n in range(len(heads)) for kb in range(8)]
    NBK = 3

    def hinfo(n):
        i, h = heads[n]
        par, hp, hb, op_ = i % 2, h // 2, (h % 2) * 64, n % 2
        num = ps[0:64, (4 + op_) * 512:(5 + op_) * 512]
        den = ps[0:64, (6 + op_) * 512:(7 + op_) * 512]
        return i, h, par, hp, hb, op_, num, den

    def prologue(n):
        i, h, par, hp, hb, op_, num, den = hinfo(n)
        shp = ps[0:1, 3 * 512:4 * 512]
        pq = (n // 2) % 2
        if h % 2 == 0:
            P.op("dve", lambda e: e.tensor_tensor(
                out=qk[pq][:], in0=Qt[par][:, hp, :], in1=Ksel[par][:, hp, 512:1024], op=ALU.mult),
                reads=[("Qt", par), ("Ksel", par)], writes=[("qk", pq)])
        P.op("pe", lambda e: e.matmul(shp, onesb[hb:hb + 64, 0:1], qk[pq][hb:hb + 64, :], start=True, stop=True),
             reads=[("qk", pq), "onesb"], writes=["shp"])
        P.op("act", lambda e: e.activation(out=sh[op_][:], in_=shp, func=AF.Identity, scale=1.0),
             reads=["shp"], writes=[("sh", op_)])

    def prologue2(n):
        i, h, par, hp, hb, op_, num, den = hinfo(n)
        P.op("pe", lambda e: e.matmul(num, zerob[:], Qt[par][:, 0, :], start=True, stop=False),
             reads=["zerob", ("Qt", par)], writes=[("num", op_)])
        P.op("pe", lambda e: e.matmul(den, zerob[:], Qt[par][:, 0, :], start=True, stop=False),
             reads=["zerob", ("Qt", par)], writes=[("den", op_)])

    def geom(kb):
        jlo, jhi = max(0, kb - 4), min(3, kb)
        return jlo, jhi, (jhi - jlo + 1) * 128, jlo * 128

    def score(s):
        n, kb = steps[s]
        i, h, par, hp, hb, op_, num, den = hinfo(n)
        jlo, jhi, ncol, q0 = geom(kb)
        bk = s % NBK
        sT = ps[:, bk * 512:bk * 512 + ncol]
        P.op("pe", lambda e: e.matmul(
            sT, Ksel[par][hb:hb + 64, hp, kb * 128:(kb + 1) * 128], Qt[par][hb:hb + 64, hp, q0:q0 + ncol], start=True, stop=False),
            reads=[("Ksel", par), ("Qt", par)], writes=[("sT", bk)])
        t0 = (4 - kb + jlo) * 128
        P.op("pe", lambda e: e.matmul(sT, K.ident[:], TTs[:, h, t0:t0 + ncol], start=False, stop=False),
             reads=["TTs", "ident"], writes=[("sT", bk)])
        if i == 0 and kb < 4:
            P.op("pe", lambda e: e.matmul(sT, K.ident[:], M0t[:, 0:ncol], start=False, stop=False),
                 reads=["M0t", "ident"], writes=[("sT", bk)])
        P.op("pe", lambda e: e.matmul(sT, K.negones[0:1, 0:128], sh[op_][0:1, q0:q0 + ncol], start=False, stop=True),
             reads=[("sh", op_), "negones"], writes=[("sT", bk)])
        P.op("act", lambda e: e.activation(out=PT[bk][:, 0:ncol], in_=sT, func=AF.Exp, scale=1.0),
             reads=[("sT", bk)], writes=[("PT", bk)])

    def pv(s):
        n, kb = steps[s]
        i, h, par, hp, hb, op_, num, den = hinfo(n)
        jlo, jhi, ncol, q0 = geom(kb)
        bk = s % NBK
        P.op("pe", lambda e: e.matmul(
            num[:, q0:q0 + ncol], Vsel[par][:, kb, h * 64:(h + 1) * 64], PT[bk][:, 0:ncol], start=False, stop=(kb == 7)),
            reads=[("Vsel", par), ("PT", bk)], writes=[("num", op_)])
        P.op("pe", lambda e: e.matmul(
            den[:, q0:q0 + ncol], onesb[:, 0:64], PT[bk][:, 0:ncol], start=False, stop=(kb == 7)),
            reads=["onesb", ("PT", bk)], writes=[("den", op_)])
        if kb == 7:
            P.op("dve", lambda e: e.reciprocal(out=rden[op_][:], in_=den),
                 reads=[("den", op_)], writes=[("rden", op_)])
            P.op("dve", lambda e: e.tensor_tensor(out=oS[op_][:], in0=num, in1=rden[op_][:], op=ALU.mult),
                 reads=[("num", op_), ("rden", op_)], writes=[("oS", op_)])
            P.dma("sp", K.attnT[h * 64:(h + 1) * 64, i * 512:(i + 1) * 512], oS[op_][:],
                  reads=[("oS", op_)], writes=[("dram", "attnA", i, h)])

    LAG = 2
    load_tile(0)
    load_tile(1)
    prologue(0)
    prologue2(0)
    ns = len(steps)
    for s in range(ns + LAG):
        if s < ns:
            n, kb = steps[s]
            i, h = heads[n]
            if kb == 3 and h == 0 and i >= 1 and i + 1 < NT:
                load_tile(i + 1)
            if kb == 2 and n + 1 < len(heads):
                prologue(n + 1)
            if kb == 6 and n + 1 < len(heads):
                prologue2(n + 1)
            score(s)
        if s - LAG >= 0:
            pv(s - LAG)


def phase_attn_b(K, l):
    nc, P, sb, ps = K.nc, K.P, K.sb, K.ps
    sb.reset(K.sb_phase_base)
    KT = [sb.alloc([128, 8192], BF16, "KT") for _ in range(2)]
    Vh = [sb.alloc([128, 64, 128], BF16, "Vh") for _ in range(2)]
    QT = [sb.alloc([128, 4096], BF16, "QT") for _ in range(2)]
    BM = sb.alloc([128, 8, 512], BF16, "BM")
    E = [sb.alloc([128, 1024], F32, "E") for _ in range(2)]
    L = [sb.alloc([128, 1024], BF16, "L") for _ in range(2)]
    W = [sb.alloc([128, 1024], BF16, "W") for _ in range(2)]
    tS = sb.alloc([128, 512], BF16, "tS")
    S32 = sb.alloc([128, 512], F32, "S32")
    Sbf = [sb.alloc([128, 512], BF16, "Sbf") for _ in range(2)]
    oS = [sb.alloc([64, 512], BF16, "oSb") for _ in range(2)]
    P.dma("sp", BM[:], K.bmask.rearrange("m b p t -> p (m b) t"), writes=["BM"])
    gv = GV(K)
    zb = ps[:, 0:1024]
    Wb = [ps[:, 1024:2048], ps[:, 2048:3072]]
    ob = [ps[0:64, 3072:3584], ps[0:64, 3584:4096]]

    groups = []
    for hp in range(4):
        for hh in range(2):
            h = hp * 2 + hh
            for i in range(NT):
                tiles = []
                for ti in range(i, -1, -1):
                    tiles.append((1, ti))
                    tiles.append((0, ti))
                seq = []
                for tn, (r, ti) in enumerate(tiles):
                    for half in (1, 0):
                        kc = r * 4096 + ti * 512 + half * 256
                        vb0 = r * 32 + ti * 4 + half * 2
                        mi = (tn * 4 + half * 2) if tn < 2 else None
                        seq.append(dict(hp=hp, hb=hh * 64, h=h, i=i, kc=kc, vb0=vb0, mi=mi))
                seq[0]["first"] = True
                seq[-1]["last"] = True
                groups += seq
    K.ngroups_b = len(groups)

    loaded = set()

    def ensure_loaded(hp):
        if hp in loaded or hp >= 4:
            return
        loaded.add(hp)
        pp = hp % 2
        for r in range(2):
            P.dma("sp", KT[pp][:, r * 4096:(r + 1) * 4096], gv.kT("b", r, hp // 2)[(hp % 2) * 128:(hp % 2 + 1) * 128, :],
                  reads=["gath"], writes=[("KT", pp, r)])
            for q4 in range(4):
                P.dma("sp", Vh[pp][:, r * 32 + q4 * 8:r * 32 + (q4 + 1) * 8, :],
                      gv.va_rows("b", r, q4 * 1024, 1024)[:, hp * 128:(hp + 1) * 128].rearrange("(b p) d -> p b d", p=128),
                      reads=["gath"], writes=[("Vh", pp, r, q4)])
        P.dma("sp", QT[pp][:], K.qbT[hp * 128:(hp + 1) * 128, :],
              reads=[("dram", "qb", i) for i in range(NT)], writes=[("QT", pp)])

    def rd_kv(g):
        pp = g["hp"] % 2
        return [("KT", pp, 0), ("KT", pp, 1), ("QT", pp)]

    def st_z(s):
        g = groups[s]
        pp, hb, i = g["hp"] % 2, g["hb"], g["i"]
        nmm = 2 + (2 if g["mi"] is not None else 0)
        for b in range(2):
            kcol = g["kc"] + (1 - b) * 128
            P.op("pe", lambda e, b=b, kcol=kcol, pp=pp, hb=hb, i=i, g=g: e.matmul(
                zb[:, b * 512:(b + 1) * 512], KT[pp][hb:hb + 64, kcol:kcol + 128], QT[pp][hb:hb + 64, i * 512:(i + 1) * 512],
                start=True, stop=(g["mi"] is None)),
                reads=rd_kv(g), writes=["zb"])
            if g["mi"] is not None:
                mi = g["mi"] + (1 - b)
                P.op("pe", lambda e, b=b, mi=mi: e.matmul(
                    zb[:, b * 512:(b + 1) * 512], K.ident[:], BM[:, mi, :], start=False, stop=True),
                    reads=["BM", "ident"], writes=["zb"])

    def st_E(s):
        q = s % 2
        P.op("act", lambda e, q=q: e.activation(out=E[q][:], in_=zb, func=AF.Exp, scale=1.0),
             reads=["zb"], writes=[("E", q)])

    def st_L(s):
        q = s % 2
        P.op("act", lambda e, q=q: e.activation(out=L[q][:], in_=E[q][:], func=AF.Ln, bias=1.0, scale=1.0),
             reads=[("E", q)], writes=[("L", q)])

    def st_S(s):
        g = groups[s]
        if g.get("last"):
            return
        q = s % 2
        nq = (s + 1) % 2
        P.op("dve", lambda e, q=q: e.tensor_tensor(out=tS[:], in0=L[q][:, 0:512], in1=L[q][:, 512:1024], op=ALU.add),
             reads=[("L", q)], writes=["tS"])
        if g.get("first"):
            P.op("dve", lambda e: e.tensor_copy(out=S32[:], in_=tS[:]), reads=["tS"], writes=["S32"])
        else:
            P.op("dve", lambda e: e.tensor_tensor(out=S32[:], in0=S32[:], in1=tS[:], op=ALU.add), reads=["tS", "S32"], writes=["S32"])
        P.op("pool", lambda e, nq=nq: e.tensor_copy(out=Sbf[nq][:], in_=S32[:]), reads=["S32"], writes=[("Sbf", nq)])

    def st_cum(s):
        g = groups[s]
        q = s % 2
        pp, hb, i = g["hp"] % 2, g["hb"], g["i"]
        first = bool(g.get("first"))
        for b in range(2):
            kcol = g["kc"] + (1 - b) * 128
            out = Wb[q][:, b * 512:(b + 1) * 512]
            P.op("pe", lambda e, out=out, kcol=kcol, pp=pp, hb=hb, i=i: e.matmul(
                out, KT[pp][hb:hb + 64, kcol:kcol + 128], QT[pp][hb:hb + 64, i * 512:(i + 1) * 512], start=True, stop=False),
                reads=rd_kv(g), writes=[("Wb", q)])
            if g["mi"] is not None:
                mi = g["mi"] + (1 - b)
                P.op("pe", lambda e, out=out, mi=mi: e.matmul(out, K.ident[:], BM[:, mi, :], start=False, stop=False),
                     reads=["BM", "ident"], writes=[("Wb", q)])
            last_here = (first and b == 0)
            P.op("pe", lambda e, out=out, q=q, b=b, last_here=last_here: e.matmul(
                out, K.trineg[:], L[q][:, b * 512:(b + 1) * 512], start=False, stop=last_here),
                reads=[("L", q), "trineg"], writes=[("Wb", q)])
            if b == 1:
                P.op("pe", lambda e, out=out, q=q, first=first: e.matmul(
                    out, K.negones[:], L[q][:, 0:512], start=False, stop=first),
                    reads=[("L", q), "negones"], writes=[("Wb", q)])
            if not first:
                P.op("pe", lambda e, out=out, q=q: e.matmul(out, K.negones[:], Sbf[q][:], start=False, stop=True),
                     reads=[("Sbf", q), "negones"], writes=[("Wb", q)])

    def st_W(s):
        q = s % 2
        P.op("act", lambda e, q=q: e.activation(out=W[q][:], in_=Wb[q], func=AF.Exp, scale=1.0),
             reads=[("Wb", q)], writes=[("W", q)])

    def st_wv(s):
        g = groups[s]
        q = s % 2
        pp, hb = g["hp"] % 2, g["hb"]
        opar = (g["h"] * NT + g["i"]) % 2
        for b in range(2):
            vblk = g["vb0"] + (1 - b)
            P.op("pe", lambda e, b=b, vblk=vblk, pp=pp, hb=hb, q=q, opar=opar, g=g: e.matmul(
                ob[opar], Vh[pp][:, vblk, hb:hb + 64], W[q][:, b * 512:(b + 1) * 512],
                start=(bool(g.get("first")) and b == 0), stop=(bool(g.get("last")) and b == 1)),
                reads=[("W", q)] + [("Vh", pp, r, q4) for r in range(2) for q4 in range(4)], writes=[("ob", opar)])
        if g.get("last"):
            h, i = g["h"], g["i"]
            P.op("dve", lambda e, opar=opar: e.tensor_copy(out=oS[opar][:], in_=ob[opar]),
                 reads=[("ob", opar)], writes=[("oSb", opar)])
            P.dma("pool", K.attnT[512 + h * 64:512 + (h + 1) * 64, i * 512:(i + 1) * 512], oS[opar][:],
                  reads=[("oSb", opar)], writes=[("dram", "attnB", i, h)])

    n = len(groups)
    ensure_loaded(0)
    st_z(0)
    st_E(0)
    if n > 1:
        st_z(1)
    for s in range(n):
        g = groups[s]
        if g.get("first") and g["i"] == 0 and g["hb"] == 0:
            ensure_loaded(g["hp"] + 1)
        if s + 1 < n:
            st_E(s + 1)
        if s + 2 < n:
            ensure_loaded(groups[s + 2]["hp"])
            st_z(s + 2)
        st_L(s)
        st_S(s)
        st_cum(s)
        if s >= 1:
            st_W(s - 1)
            st_wv(s - 1)
    st_W(n - 1)
    st_wv(n - 1)


def layer_norm_T(K, r, yout, gcol, bcol, sq, S1, S2, mean, msq, rstd, tmp, keys_r, key_y):
    P = K.P
    tkey = (lambda q: ("sq", q)) if tmp is sq else (lambda q: ("lt", q))
    for c in range(8):
        q = c % 2
        P.op("dve", lambda e, c=c, q=q: e.tensor_tensor(out=sq[q][:], in0=r[:, c, :], in1=r[:, c, :], op=ALU.mult),
             reads=[keys_r(c)], writes=[("sq", q)])
        P.op("pe", lambda e, c=c: e.matmul(S1, K.ones32[:], r[:, c, :], start=(c == 0), stop=(c == 7)),
             reads=[keys_r(c), "ones32"], writes=["S1"])
        P.op("pe", lambda e, c=c, q=q: e.matmul(S2, K.ones32[:], sq[q][:], start=(c == 0), stop=(c == 7)),
             reads=[("sq", q), "ones32"], writes=["S2"])
    P.op("dve", lambda e: e.tensor_scalar(out=mean[:], in0=S1, scalar1=1.0 / D, scalar2=None, op0=ALU.mult), reads=["S1"], writes=["mean"])
    P.op("dve", lambda e: e.tensor_tensor(out=rstd[:], in0=mean[:], in1=mean[:], op=ALU.mult), reads=["mean"], writes=["rstd"])
    P.op("dve", lambda e: e.scalar_tensor_tensor(out=rstd[:], in0=S2, scalar=1.0 / D, in1=rstd[:], op0=ALU.mult, op1=ALU.subtract),
         reads=["S2", "rstd"], writes=["rstd"])
    P.op("dve", lambda e: e.tensor_scalar(out=rstd[:], in0=rstd[:], scalar1=EPS, scalar2=None, op0=ALU.add),
         reads=["rstd"], writes=["rstd"])
    P.op("act", lambda e: e.activation(out=sq[0][:], in_=rstd[:], func=AF.Sqrt, scale=1.0),
         reads=["rstd", "S2"], writes=[("sq", 0)])
    P.op("dve", lambda e: e.reciprocal(out=rstd[:], in_=sq[0][:]),
         reads=[("sq", 0)], writes=["rstd"])
    for c in range(8):
        q = c % 2
        P.op("dve", lambda e, c=c, q=q: e.tensor_tensor(out=tmp[q][:], in0=r[:, c, :], in1=mean[:], op=ALU.subtract),
             reads=[keys_r(c), "mean"], writes=[tkey(q)])
        P.op("dve", lambda e, q=q: e.tensor_tensor(out=tmp[q][:], in0=tmp[q][:], in1=rstd[:], op=ALU.mult),
             reads=[tkey(q), "rstd"], writes=[tkey(q)])
        P.op("act", lambda e, c=c, q=q: e.activation(out=yout[:, c, :], in_=tmp[q][:], func=AF.Identity, bias=bcol[:, c:c + 1], scale=gcol[:, c:c + 1]),
             reads=[tkey(q), "lnp"], writes=[key_y(c)])


def phase_merge(K, l):
    nc, P, sb, ps = K.nc, K.P, K.sb, K.ps
    sb.reset(K.sb_phase_base)
    wpa = sb.alloc([128, 4, D], BF16, "wpa")
    wpb = sb.alloc([128, 4, D], BF16, "wpb")
    wo = sb.alloc([128, 8, D], BF16, "wo")
    stg = [sb.alloc([128, 1024], F32, "stg") for _ in range(2)]
    lng = sb.alloc([128, 8], F32, "lng")
    lnb = sb.alloc([128, 8], F32, "lnb")
    at = [sb.alloc([128, 8, 512], BF16, "at") for _ in range(2)]
    ga = [sb.alloc([128, 8, 512], BF16, "ga") for _ in range(2)]
    gb = [sb.alloc([128, 8, 512], BF16, "gb") for _ in range(2)]
    xf = [sb.alloc([128, 8, 512], F32, "xf") for _ in range(2)]
    t1 = [sb.alloc([128, 512], F32, "t1") for _ in range(2)]
    t2 = [sb.alloc([128, 512], F32, "t2") for _ in range(2)]
    mT = sb.alloc([128, 8, 512], BF16, "mT")
    rr = sb.alloc([128, 8, 512], F32, "rr")
    yo = [sb.alloc([128, 8, 512], F32, "yo") for _ in range(2)]
    sq = [sb.alloc([128, 512], F32, "sq") for _ in range(2)]
    tmp = [sb.alloc([128, 512], F32, "tmp") for _ in range(2)]
    mean = sb.alloc([128, 512], F32, "mean")
    msq = sb.alloc([128, 512], F32, "msq")
    rstd = sb.alloc([128, 512], F32, "rstd")
    xsrc = K.xT if l == 0 else K.x2T
    P.dma("sp", lng[:], K.ln1g[l], writes=["lnp"])
    P.dma("sp", lnb[:], K.ln1b[l], writes=["lnp"])
    load_cast_weight(K, K.w_pa[l], wpa, 4, D, stg, "wpa", 1)
    load_cast_weight(K, K.w_pb[l], wpb, 4, D, stg, "wpb", 1)
    load_cast_weight(K, K.w_out[l], wo, 8, D, stg, "wo", 1)
    S1 = ps[:, 6 * 512:7 * 512]
    S2 = ps[:, 7 * 512:8 * 512]
    for i in range(NT):
        par = i % 2
        tcols = slice(i * 512, (i + 1) * 512)
        P.dma("sp", at[par][:], K.attnT[:, tcols].rearrange("(k p) t -> p k t", p=128),
              reads=[("dram", "attnA", i, h) for h in range(8)] + [("dram", "attnB", i, h) for h in range(8)], writes=[("at", par)])
        P.dma("sp", ga[par][:], K.gaT[:, tcols].rearrange("(k p) t -> p k t", p=128),
              reads=[("dram", "ga0", i), ("dram", "ga1", i)], writes=[("ga", par)])
        P.dma("sp", gb[par][:], K.gbT[:, tcols].rearrange("(k p) t -> p k t", p=128),
              reads=[("dram", "gb0", i), ("dram", "gb1", i)], writes=[("gb", par)])
        P.dma("sp", xf[par][:], xsrc[:, tcols].rearrange("(k p) t -> p k t", p=128), writes=[("xf", par)])
        for c in range(8):
            q = c % 2
            pa = ps[:, (q * 2) * 512:(q * 2 + 1) * 512]
            pb = ps[:, (q * 2 + 1) * 512:(q * 2 + 2) * 512]
            for k in range(4):
                P.op("pe", lambda e, pa=pa, k=k, c=c, par=par: e.matmul(pa, wpa[:, k, c * 128:(c + 1) * 128], at[par][:, k, :], start=(k == 0), stop=(k == 3)),
                     reads=[("wpa", k), ("at", par)], writes=[("pa", q)])
            for k in range(4):
                P.op("pe", lambda e, pb=pb, k=k, c=c, par=par: e.matmul(pb, wpb[:, k, c * 128:(c + 1) * 128], at[par][:, 4 + k, :], start=(k == 0), stop=(k == 3)),
                     reads=[("wpb", k), ("at", par)], writes=[("pb", q)])
            P.op("dve", lambda e, pa=pa, c=c, q=q, par=par: e.tensor_tensor(out=t1[q][:], in0=pa, in1=ga[par][:, c, :], op=ALU.mult),
                 reads=[("pa", q), ("ga", par)], writes=[("t1", q)])
            P.op("dve", lambda e, pb=pb, c=c, q=q, par=par: e.tensor_tensor(out=t2[q][:], in0=pb, in1=gb[par][:, c, :], op=ALU.mult),
                 reads=[("pb", q), ("gb", par)], writes=[("t2", q)])
            P.op("dve", lambda e, c=c, q=q: e.tensor_tensor(out=mT[:, c, :], in0=t1[q][:], in1=t2[q][:], op=ALU.add),
                 reads=[("t1", q), ("t2", q)], writes=[("mT", c)])
        for c in range(8):
            q = c % 2
            pm = ps[:, (4 + q) * 512:(5 + q) * 512]
            for k in range(8):
                P.op("pe", lambda e, pm=pm, k=k, c=c: e.matmul(pm, wo[:, k, c * 128:(c + 1) * 128], mT[:, k, :], start=(k == 0), stop=(k == 7)),
                     reads=[("wo", k), ("mT", k)], writes=[("pm", q)])
            P.op("dve", lambda e, pm=pm, c=c, par=par: e.scalar_tensor_tensor(out=rr[:, c, :], in0=xf[par][:, c, :], scalar=ALPHA, in1=pm, op0=ALU.mult, op1=ALU.add),
                 reads=[("pm", q), ("xf", par)], writes=[("rr", c)])
        layer_norm_T(K, rr, yo[par], lng, lnb, sq, S1, S2, mean, msq, rstd, tmp,
                     keys_r=lambda c: ("rr", c), key_y=lambda c, par=par: ("yo", par, c))
        P.dma("pool", K.x1T[:, tcols].rearrange("(k p) t -> p k t", p=128), yo[par][:],
              reads=[("yo", par, c) for c in range(8)], writes=[("dram", "x1", i)])


def phase_ffn(K, l, last):
    nc, P, sb, ps = K.nc, K.P, K.sb, K.ps
    sb.reset(K.sb_phase_base)
    wf1 = sb.alloc([128, 8, 2 * DFF], BF16, "wf1")
    wf2 = sb.alloc([128, 22, D], BF16, "wf2")
    lng = sb.alloc([128, 8], F32, "lng")
    lnb = sb.alloc([128, 8], F32, "lnb")
    xfs = [sb.alloc([128, 8, 512], F32, "xf") for _ in range(2)]
    xb = sb.alloc([128, 8, 512], BF16, "xb")
    hT = sb.alloc([128, 22, 512], BF16, "hT")
    sg = [sb.alloc([128, 512], F32, "sg") for _ in range(2)]
    sq = [sb.alloc([128, 512], F32, "sq") for _ in range(2)]
    tmp = sq
    mean = sb.alloc([128, 512], F32, "mean")
    msq = None
    rstd = sb.alloc([128, 512], F32, "rstd")
    P.dma("sp", lng[:], K.ln2g[l], writes=["lnp"])
    P.dma("sp", lnb[:], K.ln2b[l], writes=["lnp"])

    def load_xf(i):
        P.dma("sp", xfs[i % 2][:], K.x1T[:, i * 512:(i + 1) * 512].rearrange("(k p) t -> p k t", p=128),
              reads=[("dram", "x1", i)], writes=[("xf", i % 2, c) for c in range(8)])

    load_xf(0)
    load_cast_weight(K, K.w_f1[l], wf1, 8, 2 * DFF, None, "wf1", 8)
    load_cast_weight(K, K.w_f2[l], wf2, 22, D, None, "wf2", 2)
    S1 = ps[:, 6 * 512:7 * 512]
    S2 = ps[:, 7 * 512:8 * 512]
    dst = K.outT if last else K.x2T
    outs = []
    for i in range(NT):
        par = i % 2
        xf = xfs[par]
        tcols = slice(i * 512, (i + 1) * 512)
        if i + 1 < NT:
            load_xf(i + 1)
        P.op("dve", lambda e, xf=xf: e.tensor_copy(out=xb[:], in_=xf[:]), reads=[("xf", par, c) for c in range(8)], writes=["xb"])
        for j in range(22):
            q = j % 2
            pg = ps[:, (q * 2) * 512:(q * 2 + 1) * 512]
            pu = ps[:, (q * 2 + 1) * 512:(q * 2 + 2) * 512]
            cg = j * 128
            cu = DFF + j * 128
            for k in range(8):
                P.op("pe", lambda e, pg=pg, k=k, cg=cg: e.matmul(pg, wf1[:, k, cg:cg + 128], xb[:, k, :], start=(k == 0), stop=(k == 7)),
                     reads=[("wf1", k), "xb"], writes=[("pg", q)])
            for k in range(8):
                P.op("pe", lambda e, pu=pu, k=k, cu=cu: e.matmul(pu, wf1[:, k, cu:cu + 128], xb[:, k, :], start=(k == 0), stop=(k == 7)),
                     reads=[("wf1", k), "xb"], writes=[("pu", q)])
            P.op("act", lambda e, pg=pg, q=q: e.activation(out=sg[q][:], in_=pg, func=AF.Silu, scale=1.0),
                 reads=[("pg", q)], writes=[("sg", q)])
            P.op("dve", lambda e, pu=pu, q=q, j=j: e.tensor_tensor(out=hT[:, j, :], in0=pu, in1=sg[q][:], op=ALU.mult),
                 reads=[("pu", q), ("sg", q)], writes=[("hT", j)])
        for c in range(8):
            q = c % 2
            pm = ps[:, (4 + q) * 512:(5 + q) * 512]
            for j in range(22):
                P.op("pe", lambda e, pm=pm, j=j, c=c: e.matmul(pm, wf2[:, j, c * 128:(c + 1) * 128], hT[:, j, :], start=(j == 0), stop=(j == 21)),
                     reads=[("wf2", j), ("hT", j)], writes=[("pm", q)])
            P.op("dve", lambda e, pm=pm, c=c, xf=xf: e.scalar_tensor_tensor(out=xf[:, c, :], in0=xf[:, c, :], scalar=ALPHA, in1=pm, op0=ALU.mult, op1=ALU.add),
                 reads=[("pm", q), ("xf", par, c)], writes=[("xf", par, c)])
        layer_norm_T(K, xf, xf, lng, lnb, sq, S1, S2, mean, msq, rstd, tmp,
                     keys_r=lambda c, par=par: ("xf", par, c), key_y=lambda c, par=par: ("xf", par, c))
        o = P.dma("sp", dst[:, tcols].rearrange("(k p) t -> p k t", p=128), xf[:],
                  reads=[("xf", par, c) for c in range(8)], writes=[("dram", "x2", i)])
        outs.append(o)
    return outs


def build(debug=False, stop_after=None):
    nc = bass.Bass("TRN2", target_bir_lowering=False)
    K = Ctx()
    K.nc = nc
    K.P = Prog()
    K.sb = SbAlloc(nc)
    K.stgn = 0

    def din(name, shape, dt=F32):
        return nc.dram_tensor(name, list(shape), dt, kind="ExternalInput").ap()

    def dscr(name, shape, dt):
        kind = "ExternalOutput" if (debug and name in debug) else "Internal"
        return nc.dram_tensor(name, list(shape), dt, kind=kind)

    K.xT = din("xT", [D, T])
    K.w_in = din("w_in", [DEPTH, D, INC])
    K.w_pa = din("w_pa", [DEPTH, 512, D])
    K.w_pb = din("w_pb", [DEPTH, 512, D])
    K.w_out = din("w_out", [DEPTH, D, D])
    K.w_f1 = din("w_f1", [DEPTH, D, 2 * DFF])
    K.w_f2 = din("w_f2", [DEPTH, DFF, D])
    K.bgate = din("bgate", [DEPTH, 128, 16])
    K.ln1g = din("ln1g", [DEPTH, 128, 8])
    K.ln1b = din("ln1b", [DEPTH, 128, 8])
    K.ln2g = din("ln2g", [DEPTH, 128, 8])
    K.ln2b = din("ln2b", [DEPTH, 128, 8])
    K.tbias = din("tbias", [DEPTH, 8, 128, 640])
    K.mband = din("mband", [128, 640])
    K.m0 = din("m0", [128, 512], BF16)
    K.sel = din("sel", [128, 2])
    K.bmask = din("bmask", [2, 4, 128, 512], BF16)
    identd = din("identd", [128, 128], BF16)
    trinegd = din("trinegd", [128, 128], BF16)
    K.outT = nc.dram_tensor("outT", [D, T], F32, kind="ExternalOutput").ap()

    K.qaT = dscr("qaT", [512, T], BF16).ap()
    K.qbT = dscr("qbT", [512, T], BF16).ap()
    K.gaT = dscr("gaT", [D, T], BF16).ap()
    K.gbT = dscr("gbT", [D, T], BF16).ap()
    K.attnT = dscr("attnT", [D, T], BF16).ap()
    K.x1T = dscr("x1T", [D, T], F32).ap()
    K.x2T = dscr("x2T", [D, T], F32).ap()
    cten = {}
    gten = {}
    for l in range(DEPTH):
        for kind in ("va", "vb", "ka", "kb"):
            for j in range(2):
                cten[(l, kind, j)] = nc.dram_tensor("c_%s%d_%d" % (kind, j, l), [2048, 512], BF16)
                gten[(l, kind, j)] = nc.dram_tensor("g_%s%d_%d" % (kind, j, l), [4096, 512], BF16)

    K.ps = nc.alloc_psum_tensor("ps", [128, 4096], F32)
    K.ident = K.sb.alloc([128, 128], BF16, "ident")
    K.trineg = K.sb.alloc([128, 128], BF16, "trineg")
    K.negones = K.sb.alloc([128, 128], BF16, "negones")
    K.ones32 = K.sb.alloc([128, 128], F32, "ones32")
    K.sb_phase_base = K.sb.off
    P = K.P
    P.dma("sp", K.ident[:], identd, writes=["ident"])
    P.dma("sp", K.trineg[:], trinegd, writes=["trineg"])
    P.op("pool", lambda e: e.memset(K.negones[:], -1.0), writes=["negones"])
    P.op("pool", lambda e: e.memset(K.ones32[:], 1.0), writes=["ones32"])

    def persist():
        pass

    outs = []
    done = False
    for l in range(DEPTH):
        K.c_va = [cten[(l, "va", j)].ap() for j in range(2)]
        K.c_vb = [cten[(l, "vb", j)].ap() for j in range(2)]
        K.c_kaT = [cten[(l, "ka", j)].ap().rearrange("(f a) b -> f (a b)", f=256) for j in range(2)]
        K.c_kbT = [cten[(l, "kb", j)].ap().rearrange("(f a) b -> f (a b)", f=256) for j in range(2)]
        K.g_va = [gten[(l, "va", j)].ap() for j in range(2)]
        K.g_vb = [gten[(l, "vb", j)].ap() for j in range(2)]
        K.g_kaT = [gten[(l, "ka", j)].ap() for j in range(2)]
        K.g_kbT = [gten[(l, "kb", j)].ap() for j in range(2)]
        K.cc_list = []
        for kind in ("va", "vb"):
            for j in range(2):
                K.cc_list.append((cten[(l, kind, j)], gten[(l, kind, j)], [("dram", kind, i) for i in range(4 * j, 4 * j + 4)]))
        for kind in ("ka", "kb"):
            for j in range(2):
                K.cc_list.append((cten[(l, kind, j)], gten[(l, kind, j)], [("dram", kind, i, j) for i in range(NT)]))
        for name, fn in (("proj", lambda: phase_proj(K, l)),
                         ("attn_a", lambda: phase_attn_a(K, l)),
                         ("attn_b", lambda: phase_attn_b(K, l)),
                         ("merge", lambda: phase_merge(K, l)),
                         ("ffn", lambda: phase_ffn(K, l, l == DEPTH - 1))):
            r = fn()
            if name == "ffn":
                outs = r
            P.barrier()
            if stop_after == (l, name):
                done = True
                break
        if done:
            break
    P.wait_all("sp", [])
    P.emit(nc)
    K.nc = nc
    return nc, K


def host_inputs(x, w_in, b_gate, rel_bias, w_proj_a, w_proj_b, w_out, ln1_g, ln1_b, w_ffn_in, w_ffn_out, ln2_g, ln2_b):
    f32 = np.float32
    common = {
        "w_in": np.ascontiguousarray(w_in, f32), "w_pa": np.ascontiguousarray(w_proj_a, f32),
        "w_pb": np.ascontiguousarray(w_proj_b, f32), "w_out": np.ascontiguousarray(w_out, f32),
        "w_f1": np.ascontiguousarray(w_ffn_in, f32), "w_f2": np.ascontiguousarray(w_ffn_out, f32),
        "bgate": np.ascontiguousarray(np.asarray(b_gate, f32).reshape(DEPTH, 16, 128).transpose(0, 2, 1)),
        "ln1g": np.ascontiguousarray(np.asarray(ln1_g, f32).reshape(DEPTH, 8, 128).transpose(0, 2, 1)),
        "ln1b": np.ascontiguousarray(np.asarray(ln1_b, f32).reshape(DEPTH, 8, 128).transpose(0, 2, 1)),
        "ln2g": np.ascontiguousarray(np.asarray(ln2_g, f32).reshape(DEPTH, 8, 128).transpose(0, 2, 1)),
        "ln2b": np.ascontiguousarray(np.asarray(ln2_b, f32).reshape(DEPTH, 8, 128).transpose(0, 2, 1)),
    }
    p_ = np.arange(128)[:, None]
    col = np.arange(640)[None, :]
    sig = col // 128
    qq = col % 128
    delta = 4 - sig
    idx = np.clip(512 - 128 * delta + qq - p_, -256, 256) + 256
    common["tbias"] = np.ascontiguousarray(np.asarray(rel_bias, f32)[:, :, idx])
    dchunk = 8 - 2 * delta + qq // 64 - p_ // 64
    valid = (dchunk >= 0) & (dchunk <= 8)
    common["mband"] = np.where(valid, 0.0, NEG).astype(f32)
    common["identd"] = np.eye(128, dtype=f32).astype(ml_dtypes.bfloat16)
    jj = np.arange(128)[:, None]
    ss = np.arange(128)[None, :]
    common["trinegd"] = np.where(jj >= ss, -1.0, 0.0).astype(f32).astype(ml_dtypes.bfloat16)
    kp = (np.arange(4)[:, None, None] * 128 + np.arange(128)[None, :, None])
    tq = np.arange(512)[None, None, :]
    causal = np.where(kp >= tq, NEG, 0.0).astype(f32)
    full = np.full((4, 128, 512), NEG, f32)
    none = np.zeros((4, 128, 512), f32)
    bm = [np.stack([full, causal]), np.stack([causal, none])]
    m0 = [np.full((128, 512), NEG, f32).astype(ml_dtypes.bfloat16), np.zeros((128, 512), f32).astype(ml_dtypes.bfloat16)]
    sel = [np.tile(np.array([[1.0, 0.0]], f32), (128, 1)), np.tile(np.array([[0.0, 1.0]], f32), (128, 1))]
    in_maps = []
    x = np.asarray(x, f32)
    for core in range(8):
        b, r = core // 2, core % 2
        xt = x[b].reshape(16, 512, D)[r::2].reshape(T, D)
        m = dict(common)
        m["xT"] = np.ascontiguousarray(xt.T)
        m["bmask"] = bm[r].astype(ml_dtypes.bfloat16)
        m["m0"] = m0[r]
        m["sel"] = sel[r]
        in_maps.append(m)
    return in_maps


_CACHE = {}


def kernel(**inputs):
    in_maps = host_inputs(**inputs)
    if "nc" not in _CACHE:
        _CACHE["nc"] = build()[0]
    nc = _CACHE["nc"]
    res = run_bass_kernel_spmd(nc, in_maps, core_ids=list(range(8)))
    out = np.empty((NB, S, D), np.float32)
    for core in range(8):
        b, r = core // 2, core % 2
        o = np.asarray(res.results[core]["outT"], np.float32).T.reshape(8, 512, D)
        out[b].reshape(16, 512, D)[r::2] = o
    return out
```
